# Optimizing a Trainium2 kernel written in Bass

```python
import math
import jax, jax.numpy as jnp
from jax import lax
import numpy as np

D_MODEL = 2048
BATCH = 2
SEQ = 4096
DEPTH = 1
DEC_BATCH = 4
DEC_SEQ = 2048
PAST_LEN = 128

MIX_WIDTH = D_MODEL
ATTN_WIDTH = MIX_WIDTH // 2
FOURIER_WIDTH = MIX_WIDTH - ATTN_WIDTH
HEAD_DIM = 128
N_HEADS = ATTN_WIDTH // HEAD_DIM
N_FOURIER_GROUPS = 4
FOURIER_GROUP_DIM = FOURIER_WIDTH // N_FOURIER_GROUPS
ROPE_THETA = 500000.0
ROPE_DIM = HEAD_DIM // 4
DILATED_PATTERNS = ((128, 1), (512, 4), (2048, 16))
QUERY_BLOCK = 128
RMS_EPS = 1e-6
IN_WIDTH = 4 * ATTN_WIDTH + 2 * FOURIER_WIDTH

kernel_name = 'hybrid_fourier_dilated_encoder'


def _rmsnorm(x, gain):
    xf = x.astype(jnp.float32)
    y = xf * lax.rsqrt(jnp.mean(xf * xf, axis=-1, keepdims=True) + RMS_EPS) * gain.astype(jnp.float32)
    return y.astype(x.dtype)


def _partial_rope(x):
    S = x.shape[1]
    half = ROPE_DIM // 2
    inv_freq = 1.0 / (ROPE_THETA ** (jnp.arange(half, dtype=jnp.float32) / half))
    ang = jnp.arange(S, dtype=jnp.float32)[:, None] * inv_freq[None, :]
    cos = jnp.cos(ang)[None, :, None, :]
    sin = jnp.sin(ang)[None, :, None, :]
    xf = x.astype(jnp.float32)
    x1, x2, rest = xf[..., :half], xf[..., half:ROPE_DIM], xf[..., ROPE_DIM:]
    rot = jnp.concatenate([x1 * cos - x2 * sin, x1 * sin + x2 * cos, rest], axis=-1)
    return rot.astype(x.dtype)


def _dilated_window_branch(q, k, v, window, dilation):
    B, S, H, Dh = q.shape
    half = window // (2 * dilation)
    L = S // dilation

    def to_sub(t):
        return t.reshape(B, L, dilation, H, Dh).transpose(0, 2, 1, 3, 4)

    qs, ks, vs = to_sub(q), to_sub(k), to_sub(v)
    qb = math.gcd(L, QUERY_BLOCK)
    nb = L // qb
    span = qb + 2 * half
    pad = ((0, 0), (0, 0), (half, half), (0, 0), (0, 0))
    kp, vp = jnp.pad(ks, pad), jnp.pad(vs, pad)
    key_idx = jnp.arange(nb)[:, None] * qb + jnp.arange(span)[None, :]
    kb = kp[:, :, key_idx]
    vb = vp[:, :, key_idx]
    qblk = qs.reshape(B, dilation, nb, qb, H, Dh)
    scale = HEAD_DIM ** -0.5
    scores = jnp.einsum('brnqhd,brnkhd->brnhqk', qblk, kb).astype(jnp.float32) * scale
    rel = jnp.arange(span)[None, :] - half - jnp.arange(qb)[:, None]
    key_pos = key_idx - half
    valid = (jnp.abs(rel) <= half)[None] & ((key_pos >= 0) & (key_pos < L))[:, None, :]
    scores = jnp.where(valid[None, None, :, None, :, :], scores, -jnp.inf)
    m = jnp.max(scores, axis=-1, keepdims=True)
    p = jnp.exp(scores - m)
    s = jnp.sum(p, axis=-1, keepdims=True)
    o = jnp.einsum('brnhqk,brnkhd->brnqhd', p / s, vb.astype(jnp.float32))
    lse = (m + jnp.log(s))[..., 0].transpose(0, 1, 2, 4, 3)
    o = o.reshape(B, dilation, L, H, Dh).transpose(0, 2, 1, 3, 4).reshape(B, S, H, Dh)
    lse = lse.reshape(B, dilation, L, H).transpose(0, 2, 1, 3).reshape(B, S, H)
    return o, lse


def _dilated_attention(q, k, v):
    outs, lses = [], []
    for window, dilation in DILATED_PATTERNS:
        o, lse = _dilated_window_branch(q, k, v, window, dilation)
        outs.append(o)
        lses.append(lse)
    w = jax.nn.softmax(jnp.stack(lses, axis=0), axis=0)
    return jnp.einsum('pbsh,pbshd->bshd', w, jnp.stack(outs, axis=0))


def _fourier_mix(u, w_fourier):
    B, S, _ = u.shape
    ug = u.reshape(B, S, N_FOURIER_GROUPS, FOURIER_GROUP_DIM).astype(jnp.float32)
    f = jnp.fft.fftn(ug, axes=(1, 3), norm='ortho').real
    f = jnp.einsum('bsgc,gce->bsge', f, w_fourier.astype(jnp.float32))
    return f.reshape(B, S, FOURIER_WIDTH)


def _layer(x, rms_gain, w_in, q_norm_gain, k_norm_gain, w_fourier, w_out):
    B, S, _ = x.shape
    h = _rmsnorm(x, rms_gain)
    proj = jnp.einsum('bsd,de->bse', h, w_in)
    A, F = ATTN_WIDTH, FOURIER_WIDTH
    q, k, v, g_a, u_f, g_f = jnp.split(proj, [A, 2 * A, 3 * A, 4 * A, 4 * A + F], axis=-1)
    q = _partial_rope(_rmsnorm(q.reshape(B, S, N_HEADS, HEAD_DIM), q_norm_gain))
    k = _partial_rope(_rmsnorm(k.reshape(B, S, N_HEADS, HEAD_DIM), k_norm_gain))
    v = v.reshape(B, S, N_HEADS, HEAD_DIM)
    attn = _dilated_attention(q, k, v).reshape(B, S, A)
    attn = attn * jax.nn.silu(g_a.astype(jnp.float32))
    four = _fourier_mix(u_f, w_fourier) * jax.nn.silu(g_f.astype(jnp.float32))
    mix = jnp.concatenate([attn, four], axis=-1).astype(x.dtype)
    return x + jnp.einsum('bse,ed->bsd', mix, w_out)


def _trunk(x, rms_gain, w_in, q_norm_gain, k_norm_gain, w_fourier, w_out):
    for l in range(DEPTH):
        x = _layer(x, rms_gain[l], w_in[l], q_norm_gain[l], k_norm_gain[l], w_fourier[l], w_out[l])
    return x


def setup_inputs(seed: int = 0) -> dict:
    key = jax.random.key(seed)
    ks = jax.random.split(key, 8)
    nrm = jax.random.normal
    f32 = jnp.float32
    return {
        'x_prompt': nrm(ks[0], (BATCH, SEQ, D_MODEL), f32),
        'x_sample': nrm(ks[1], (DEC_BATCH, DEC_SEQ, D_MODEL), f32),
        'rms_gain': 1.0 + 0.02 * nrm(ks[2], (DEPTH, D_MODEL), f32),
        'w_in': nrm(ks[3], (DEPTH, D_MODEL, IN_WIDTH), f32) * (D_MODEL ** -0.5),
        'q_norm_gain': 1.0 + 0.02 * nrm(ks[4], (DEPTH, HEAD_DIM), f32),
        'k_norm_gain': 1.0 + 0.02 * nrm(ks[5], (DEPTH, HEAD_DIM), f32),
        'w_fourier': nrm(ks[6], (DEPTH, N_FOURIER_GROUPS, FOURIER_GROUP_DIM, FOURIER_GROUP_DIM), f32) * (FOURIER_GROUP_DIM ** -0.5),
        'w_out': nrm(ks[7], (DEPTH, MIX_WIDTH, D_MODEL), f32) * (MIX_WIDTH ** -0.5),
    }


def reference(x_prompt, x_sample, rms_gain, w_in, q_norm_gain, k_norm_gain, w_fourier, w_out):
    y_prompt = _trunk(x_prompt, rms_gain, w_in, q_norm_gain, k_norm_gain, w_fourier, w_out)
    y_sample = _trunk(x_sample, rms_gain, w_in, q_norm_gain, k_norm_gain, w_fourier, w_out)
    return (y_prompt, y_sample)
```

```python
import math
from contextlib import ExitStack

import numpy as np
import ml_dtypes

import concourse.bass as bass
import concourse.mybir as mybir
from concourse.bass_utils import run_bass_kernel_spmd

F32 = mybir.dt.float32
BF16 = mybir.dt.bfloat16
AF = mybir.ActivationFunctionType
ALU = mybir.AluOpType

NT = 2048
DM = 2048
EPS = 1e-6
ATT_SCALE = 128 ** -0.5
BF = ml_dtypes.bfloat16


class Ev:
    __slots__ = ("sem", "val", "sid")

    def __init__(self, sem, val, sid):
        self.sem = sem
        self.val = val
        self.sid = sid


class Res:
    __slots__ = ("name", "w", "r")

    def __init__(self, name):
        self.name = name
        self.w = None
        self.r = {}


class Eng:
    def __init__(self, fw, name, is_pe=False):
        self.name = name
        self.is_pe = is_pe
        self.sem = fw.new_sem("e_" + name)
        self.sid = fw.sid(self.sem)
        self.count = 0
        self.seen = {}
        self.prog = []
        self.nins = 0


class FW:
    def __init__(self, nc, stack):
        self.nc = nc
        self.stack = stack
        self._sids = {}
        self.nsem = 0
        self.pe = Eng(self, "tensor", is_pe=True)
        self.act = Eng(self, "scalar")
        self.dve = Eng(self, "vector")
        self.pool = Eng(self, "gpsimd")
        self.sp = Eng(self, "sync")
        self.engs = [self.pe, self.act, self.dve, self.pool, self.sp]
        self.dsems = {}
        self.dcount = {}

    def new_sem(self, name):
        self.nsem += 1
        return self.stack.enter_context(self.nc.semaphore(name))

    def sid(self, sem):
        k = id(sem)
        if k not in self._sids:
            self._sids[k] = len(self._sids)
        return self._sids[k]

    def res(self, name="r"):
        return Res(name)

    def _wait(self, eng, ev):
        if ev is None:
            return
        if eng.is_pe and ev.sid == eng.sid:
            return
        if eng.seen.get(ev.sid, 0) >= ev.val:
            return
        eng.seen[ev.sid] = ev.val
        sem, val = ev.sem, ev.val
        eng.prog.append(lambda h: h.wait_ge(sem, val))

    def _deps(self, eng, reads, writes):
        for r in reads:
            self._wait(eng, r.w)
        for w in writes:
            self._wait(eng, w.w)
            for e in w.r.values():
                self._wait(eng, e)

    def _commit(self, ev, reads, writes):
        for r in reads:
            old = r.r.get(ev.sid)
            if old is None or old.val < ev.val:
                r.r[ev.sid] = ev
        for w in writes:
            w.w = ev
            w.r = {}

    def op(self, eng, fn, reads=(), writes=(), inc=True):
        self._deps(eng, reads, writes)
        eng.nins += 1
        if inc:
            eng.count += 1
            ev = Ev(eng.sem, eng.count, eng.sid)
            sem = eng.sem
            eng.prog.append(lambda h: fn(h).then_inc(sem, 1))
        else:
            ev = Ev(eng.sem, eng.count + 1, eng.sid)
            eng.prog.append(lambda h: fn(h))
        self._commit(ev, reads, writes)
        return ev

    def dma(self, q, out, in_, reads=(), writes=(), key=None, **kw):
        self._deps(q, reads, writes)
        if key not in self.dsems:
            self.dsems[key] = self.new_sem("d_" + key)
            self.dcount[key] = 0
        sem = self.dsems[key]
        self.dcount[key] += 16
        ev = Ev(sem, self.dcount[key], self.sid(sem))
        q.nins += 1
        q.prog.append(lambda h: h.dma_start(out=out, in_=in_, **kw).then_inc(sem, 16))
        self._commit(ev, reads, writes)
        return ev

    def dma_fn(self, q, fn, reads=(), writes=(), key=None):
        self._deps(q, reads, writes)
        if key not in self.dsems:
            self.dsems[key] = self.new_sem("d_" + key)
            self.dcount[key] = 0
        sem = self.dsems[key]
        self.dcount[key] += 16
        ev = Ev(sem, self.dcount[key], self.sid(sem))
        q.nins += 1
        q.prog.append(lambda h: fn(h).then_inc(sem, 16))
        self._commit(ev, reads, writes)
        return ev

    def barrier(self):
        evs = []
        for key, sem in self.dsems.items():
            evs.append(Ev(sem, self.dcount[key], self.sid(sem)))
        for e in self.engs:
            if e.count > 0:
                evs.append(Ev(e.sem, e.count, e.sid))
        for e in self.engs:
            for ev in evs:
                if ev.sid != e.sid:
                    self._wait(e, ev)

    def flush(self):
        nc = self.nc
        for key, sem in self.dsems.items():
            self._wait(self.sp, Ev(sem, self.dcount[key], self.sid(sem)))
        for e in self.engs:
            if e is not self.sp and e.count > 0:
                self._wait(self.sp, Ev(e.sem, e.count, e.sid))
        progs = {e.name: e.prog for e in self.engs}
        for e in self.engs:
            e.prog = []

        def run(name):
            def body(h):
                for c in progs[name]:
                    c(h)
            return body

        with nc.Block(no_gpsimd_drain=True) as block:
            block.tensor(run("tensor"))
            block.scalar(run("scalar"))
            block.vector(run("vector"))
            block.gpsimd(run("gpsimd"))
            block.sync(run("sync"))


def build(debug=False):
    nc = bass.Bass("TRN2", target_bir_lowering=False)

    def D(name, shape, dt, kind="ExternalInput"):
        return nc.dram_tensor(name, shape, dt, kind=kind).ap()

    x_own = D("x_own", [NT, DM], F32)
    x_ext = D("x_ext", [NT, DM], F32)
    hoff_d = D("hoff", [1, 1], mybir.dt.int32)
    rms_g = D("rms_g", [1, DM], F32)
    w_in = D("w_in", [DM, 6144], F32)
    qg = D("qg", [128, 1], F32)
    kg = D("kg", [128, 1], F32)
    w_f = D("w_f", [4, 256, 256], F32)
    w_out = D("w_out", [DM, DM], F32)
    ident_d = D("ident", [128, 128], BF16)
    ones_d = D("ones", [128, 128], BF16)
    pswap_d = D("pswap", [128, 128], BF16)
    rope_own = D("rope_own", [2, 32, NT], F32)
    rope_oth = D("rope_oth", [2, 32, NT], F32)
    masks_d = D("masks", [128, 2, 256], BF16)
    dft_d = D("dft", [2, 4, 128, 16, 2, 256], BF16)
    csm_d = D("csm", [128, 2, 2, 256], BF16)
    y = D("y", [NT, DM], F32, kind="ExternalOutput")

    sk = "ExternalOutput" if debug else "Internal"
    KS = D("KS", [8, 128, 4096], BF16, kind=sk)
    VS = D("VS", [8, 128, 4096], BF16, kind=sk)
    QS = D("QS", [8, 128, NT], BF16, kind=sk)
    GAS = D("GAS", [8, 128, NT], BF16, kind=sk)
    GFS = D("GFS", [8, 128, NT], BF16, kind=sk)
    UOS = D("UOS", [16, 128, 1024], BF16, kind=sk)
    SDS = D("SDS", [2, 16, 128, 1024], BF16, kind=sk)
    DEN1 = D("DEN1", [8, 2048], F32, kind=sk)
    DEN2 = D("DEN2", [8, 2048], F32, kind=sk)
    MIXS = D("MIXS", [16, 128, NT], BF16, kind="ExternalOutput") if debug else None

    with ExitStack() as st:
        fw = FW(nc, st)
        pe, act, dve, pool, sp = fw.pe, fw.act, fw.dve, fw.pool, fw.sp

        def mm(out, lhsT, rhs, start, stop, R=(), W=(), inc=True):
            return fw.op(pe, lambda h: h.matmul(out, lhsT=lhsT, rhs=rhs, start=start, stop=stop), R, W, inc)

        def tr(out, in_, idn, R=(), W=(), inc=True):
            return fw.op(pe, lambda h: h.transpose(out=out, in_=in_, identity=idn), R, W, inc)

        def actv(out, in_, func, R=(), W=(), scale=None, bias=None, accum=None):
            kw = {}
            if scale is not None:
                kw["scale"] = scale
            if bias is not None:
                kw["bias"] = bias
            if accum is not None:
                kw["accum_out"] = accum
            return fw.op(act, lambda h: h.activation(out=out, in_=in_, func=func, **kw), R, W)

        def tt(eng, out, a, b, op, R=(), W=()):
            return fw.op(eng, lambda h: h.tensor_tensor(out=out, in0=a, in1=b, op=op), R, W)

        def stt(out, in0, scalar, in1, op0, op1, R=(), W=()):
            return fw.op(dve, lambda h: h.scalar_tensor_tensor(out=out, in0=in0, scalar=scalar, in1=in1,
                                                               op0=op0, op1=op1), R, W)

        def cp(eng, out, in_, R=(), W=()):
            if eng is act:
                return fw.op(act, lambda h: h.copy(out=out, in_=in_), R, W)
            return fw.op(eng, lambda h: h.tensor_copy(out=out, in_=in_), R, W)

        def recip(out, in_, R=(), W=()):
            return fw.op(dve, lambda h: h.reciprocal(out=out, in_=in_), R, W)

        def ld(out, in_, W=(), key=None, **kw):
            return fw.dma(sp, out, in_, (), W, key=key, **kw)

        def stq(out, in_, R=(), key=None, **kw):
            return fw.dma(sp, out, in_, R, (), key=key, **kw)

        uniq = [0]

        def SB(stack, name, shape, dt):
            uniq[0] += 1
            return stack.enter_context(nc.sbuf_tensor("sb%d_%s" % (uniq[0], name), shape, dt))

        def PS(stack, name, shape, dt):
            uniq[0] += 1
            return stack.enter_context(nc.psum_tensor("ps%d_%s" % (uniq[0], name), shape, dt))

        ident = SB(st, "ident", [128, 128], BF16)
        ones = SB(st, "ones", [128, 128], BF16)
        pswap = SB(st, "pswap", [128, 128], BF16)
        qg_t = SB(st, "qg_t", [128, 1], F32)
        kg_t = SB(st, "kg_t", [128, 1], F32)
        r_const = fw.res("const")
        ld(ident[:], ident_d, [r_const], key="c0")
        ld(ones[:], ones_d, [r_const], key="c0")
        ld(pswap[:], pswap_d, [r_const], key="c0")
        ld(qg_t[:], qg, [r_const], key="c0")
        ld(kg_t[:], kg, [r_const], key="c0")

        def run_pipeline(items, skew):
            n = len(items)
            ns = len(skew)
            for step in range(n + max(skew)):
                for s in range(ns):
                    b = step - skew[s]
                    if 0 <= b < n and s < len(items[b]) and items[b][s] is not None:
                        items[b][s]()

        def ldw(out, in_, W=(), key=None):
            return fw.dma(pool, out, in_, (), W, key=key)

        def phase_norm(x_src, hT, ntiles, col0=0):
            with ExitStack() as s2:
                NXT = 3
                xt = [SB(s2, "xt%d" % i, [128, DM], F32) for i in range(NXT)]
                r_xt = [fw.res() for _ in range(NXT)]
                junk = SB(s2, "junk", [128, DM], BF16)
                r_junk = fw.res()
                xn = [SB(s2, "xn%d" % i, [128, DM], BF16) for i in range(NXT)]
                r_xn = [fw.res() for _ in range(NXT)]
                gb = SB(s2, "gb", [128, DM], F32)
                r_gb = fw.res()
                ss = [SB(s2, "ss%d" % i, [128, 1], F32) for i in range(NXT)]
                r_ss = [fw.res() for _ in range(NXT)]
                ptb = [PS(s2, "ptb%d" % i, [128, 8, 128], BF16) for i in range(6)]
                r_ptb = [fw.res() for _ in range(6)]
                ld(gb[:], rms_g.broadcast_to([128, DM]), [r_gb], key="gb")
                items = []
                for t in range(ntiles):
                    sl = t % NXT

                    def s0(t=t, sl=sl):
                        ld(xt[sl][:], x_src[128 * t:128 * t + 128, :], [r_xt[sl]], key="xt%d" % sl)

                    def s1(t=t, sl=sl):
                        actv(junk[:], xt[sl][:], AF.Square, [r_xt[sl]], [r_junk, r_ss[sl]], accum=ss[sl][:])
                        actv(ss[sl][:], ss[sl][:], AF.Sqrt, [r_ss[sl]], [r_ss[sl]], scale=1.0 / DM, bias=EPS)
                        recip(ss[sl][:], ss[sl][:], [r_ss[sl]], [r_ss[sl]])
                        stt(xn[sl][:], xt[sl][:], ss[sl][:], gb[:], ALU.mult, ALU.mult,
                            [r_xt[sl], r_ss[sl], r_gb], [r_xn[sl]])

                    def s2_(t=t, sl=sl):
                        for half in range(2):
                            bi = (2 * t + half) % 6
                            for c in range(8):
                                cc = 8 * half + c
                                tr(ptb[bi][:, c, :], xn[sl][:, cc * 128:(cc + 1) * 128], ident[:],
                                   [r_xn[sl], r_const], [r_ptb[bi]], inc=(c == 7))

                    def s3(t=t, sl=sl):
                        for half in range(2):
                            bi = (2 * t + half) % 6
                            cp(act if half == 0 else dve, hT[:, 8 * half:8 * half + 8, col0 + 128 * t:col0 + 128 * t + 128],
                               ptb[bi][:], [r_ptb[bi]], [])

                    items.append([s0, s1, s2_, s3])
                run_pipeline(items, [0, 1, 2, 3])
                fw.barrier()

        NW = 4

        NPRE = 1

        def prefetch_units(units, wbf, r_wbf):
            for i in range(min(NPRE, len(units))):
                kind, idx, col0 = units[i]
                ldw(wbf[i % NW][:], w_in[:, col0:col0 + 256].rearrange("(c p) n -> p c n", p=128),
                    [r_wbf[i % NW]], key="wbf%d" % (i % NW))

        def phase_proj(hT, units, rope_d, own, wbf, r_wbf):
            with ExitStack() as s2:
                cosT = SB(s2, "cosT", [32, NT], F32)
                sinT = SB(s2, "sinT", [32, NT], F32)
                r_rope = fw.res()
                ld(cosT[:], rope_d[0], [r_rope], key="rope")
                ld(sinT[:], rope_d[1], [r_rope], key="rope")
                NS = 3
                NO = 5
                sq = [SB(s2, "sq%d" % i, [128, 512], BF16) for i in range(NS)]
                r_sq = [fw.res() for _ in range(NS)]
                rs = [SB(s2, "rs%d" % i, [128, 512], F32) for i in range(NS)]
                r_rs = [fw.res() for _ in range(NS)]
                ot = [SB(s2, "ot%d" % i, [128, 512], BF16) for i in range(NO)]
                r_ot = [fw.res() for _ in range(NO)]
                t1 = [SB(s2, "t1_%d" % i, [32, 512], F32) for i in range(NS)]
                r_t1 = [fw.res() for _ in range(NS)]
                t2 = [SB(s2, "t2_%d" % i, [32, 512], F32) for i in range(NS)]
                r_t2 = [fw.res() for _ in range(NS)]
                ub = [SB(s2, "ub%d" % i, [128, 16, 256], BF16) for i in range(2)]
                r_ub = [[fw.res() for _ in range(2)] for _ in range(2)]
                uo2 = [SB(s2, "uo%d" % i, [128, 16, 256], BF16) for i in range(2)]
                r_uo2 = [fw.res() for _ in range(2)]
                tok0 = 0 if own else 1024

                def load_uo(j):
                    kind_, idx_, _c = units[j]
                    ld(uo2[idx_ % 2][:], UOS[:, :, 256 * idx_:256 * idx_ + 256].rearrange("t p c -> p t c"),
                       [r_uo2[idx_ % 2]], key="uo%d" % (idx_ % 2))
                NA = 4
                pacc = [PS(s2, "pacc%d" % i, [128, 512], F32) for i in range(NA)]
                r_pacc = [fw.res() for _ in range(NA)]
                pss = [PS(s2, "pss%d" % i, [128, 512], F32) for i in range(2)]
                r_pss = [fw.res() for _ in range(2)]
                ppq = [PS(s2, "ppq%d" % i, [128, 512], F32) for i in range(2)]
                r_ppq = [fw.res() for _ in range(2)]
                cnt = {"acc": 0, "pp": 0, "ot": 0}

                def load_unit(i):
                    kind, idx, col0 = units[i]
                    sl = i % NW
                    ldw(wbf[sl][:], w_in[:, col0:col0 + 256].rearrange("(c p) n -> p c n", p=128),
                        [r_wbf[sl]], key="wbf%d" % sl)

                for i in range(NPRE, min(NW - 1, len(units))):
                    load_unit(i)
                r_halo = fw.res()
                if not own:
                    state = {}

                    def halo_dma(h, c4):
                        if "c" not in state:
                            reg = h.alloc_register("hoff")
                            h.reg_load(reg, hoff_d[0:1, 0:1])
                            state["c"] = h.snap(reg, min_val=1024, max_val=2048)
                        return h.dma_start(out=hT[:, 4 * c4:4 * c4 + 4, 0:1024],
                                           in_=hT[:, 4 * c4:4 * c4 + 4, bass.ds(state["c"], 1024)])

                    for c4 in range(4):
                        fw.dma_fn(sp, lambda h, c4=c4: halo_dma(h, c4), (), [r_halo], key="halo")
                items = []
                for i, (kind, idx, col0) in enumerate(units):
                    sl = i % NW
                    first_block = True
                    if kind == "u":
                        cs = slice(256 * idx, 256 * idx + 256)
                        for t in range(16):
                            a = cnt["acc"] % NA
                            cnt["acc"] += 1

                            def s0(i=i, sl=sl, t=t, a=a, fb=first_block, idx=idx):
                                if fb:
                                    if i + NW - 1 < len(units):
                                        load_unit(i + NW - 1)
                                    if own and i + 1 < len(units) and units[i + 1][0] == "u":
                                        load_uo(i + 1)
                                tk = tok0 + 128 * t
                                for c in range(16):
                                    mm(pacc[a][:, 0:256], hT[:, c, tk:tk + 128], wbf[sl][:, c, :],
                                       c == 0, c == 15, [r_wbf[sl]], [r_pacc[a]], inc=(c == 15))

                            def s1(t=t, a=a, cs=cs, idx=idx):
                                uo, r_uo = uo2[idx % 2], r_uo2[idx % 2]
                                if own:
                                    tt(dve, ub[0][:, t, :], pacc[a][:, 0:256], uo[:, t, :], ALU.add,
                                       [r_pacc[a], r_uo], [r_ub[0][t // 8]])
                                    tt(dve, ub[1][:, t, :], pacc[a][:, 0:256], uo[:, t, :], ALU.subtract,
                                       [r_pacc[a], r_uo], [r_ub[1][t // 8]])
                                else:
                                    cp(act, ub[0][:, t, :], pacc[a][:, 0:256], [r_pacc[a]], [r_ub[0][t // 8]])
                                if t % 8 == 7:
                                    hs = slice(t - 7, t + 1)
                                    hh = t // 8
                                    if own:
                                        stq(SDS[0, hs, :, cs].rearrange("t p c -> p t c"), ub[0][:, hs, :], [r_ub[0][hh]], key="ub0%d" % hh)
                                        stq(SDS[1, hs, :, cs].rearrange("t p c -> p t c"), ub[1][:, hs, :], [r_ub[1][hh]], key="ub1%d" % hh)
                                    else:
                                        stq(UOS[hs, :, cs].rearrange("t p c -> p t c"), ub[0][:, hs, :], [r_ub[0][hh]], key="ub0%d" % hh)

                            items.append([s0, s1])
                            first_block = False
                        continue
                    for sub in range(2):
                        head = 2 * idx + sub
                        for b in range(4 if own else 2):
                            a = cnt["acc"] % NA
                            cnt["acc"] += 1
                            o = cnt["ot"] % NO
                            cnt["ot"] += 1
                            p = cnt["pp"] % NS
                            p2 = cnt["pp"] % 2
                            if kind in ("q", "k"):
                                cnt["pp"] += 1
                            tsl = slice(512 * b, 512 * b + 512)
                            if kind == "q":
                                dst = QS[head, :, tsl]
                            elif kind == "ga":
                                dst = GAS[head, :, tsl]
                            elif kind == "gf":
                                dst = GFS[head, :, tsl]
                            else:
                                T = KS if kind == "k" else VS
                                if own:
                                    dst = T[head, :, 1024 + 512 * b:1024 + 512 * b + 512]
                                else:
                                    dst = (T[head, :, 512 * b:512 * b + 512],
                                           T[head, :, 3072 + 512 * b:3072 + 512 * b + 512])
                            gain = qg_t if kind == "q" else kg_t

                            def s0(i=i, sl=sl, sub=sub, b=b, a=a, fb=first_block):
                                if fb and i + NW - 1 < len(units):
                                    load_unit(i + NW - 1)
                                if fb and own and i + 1 < len(units) and units[i + 1][0] == "u":
                                    load_uo(i + 1)
                                for c in range(16):
                                    mm(pacc[a][:], wbf[sl][:, c, 128 * sub:128 * sub + 128],
                                       hT[:, c, 512 * b:512 * b + 512],
                                       c == 0, c == 15, [r_wbf[sl], r_halo], [r_pacc[a]], inc=(c == 15))

                            if kind in ("q", "k"):
                                def s1(a=a, p=p):
                                    actv(sq[p][:], pacc[a][:], AF.Square, [r_pacc[a]], [r_sq[p]])

                                def s2_(p=p, p2=p2):
                                    mm(pss[p2][:], ones[:], sq[p][:], True, True, [r_sq[p], r_const], [r_pss[p2]])

                                def s3(a=a, p=p, p2=p2, o=o, gain=gain):
                                    actv(rs[p][:], pss[p2][:], AF.Ln, [r_pss[p2]], [r_rs[p]], scale=1.0 / 128, bias=EPS)
                                    actv(rs[p][:], rs[p][:], AF.Exp, [r_rs[p]], [r_rs[p]], scale=-0.5)
                                    stt(ot[o][:], pacc[a][:], gain[:], rs[p][:], ALU.mult, ALU.mult,
                                        [r_pacc[a], r_rs[p], r_const], [r_ot[o]])

                                def s4(p2=p2, o=o):
                                    mm(ppq[p2][:], pswap[:], ot[o][:], True, True, [r_ot[o], r_const], [r_ppq[p2]])

                                def s5(p=p, p2=p2, o=o, tsl=tsl, dst=dst):
                                    tt(pool, t1[p][:], ot[o][0:32, :], cosT[:, tsl], ALU.mult,
                                       [r_ot[o], r_rope], [r_t1[p]])
                                    tt(dve, t2[p][:], ppq[p2][0:32, :], sinT[:, tsl], ALU.mult,
                                       [r_ppq[p2], r_rope], [r_t2[p]])
                                    tt(pool, ot[o][0:32, :], t1[p][:], t2[p][:], ALU.add,
                                       [r_t1[p], r_t2[p]], [r_ot[o]])
                                    for dd in (dst if isinstance(dst, tuple) else (dst,)):
                                        stq(dd, ot[o][:], [r_ot[o]], key="ot%d" % o)

                                items.append([s0, s1, s2_, s3, s4, s5])
                            else:
                                def s1(a=a, o=o, kind=kind, dst=dst):
                                    if kind == "v":
                                        cp(act, ot[o][:], pacc[a][:], [r_pacc[a]], [r_ot[o]])
                                    else:
                                        actv(ot[o][:], pacc[a][:], AF.Silu, [r_pacc[a]], [r_ot[o]])
                                    for dd in (dst if isinstance(dst, tuple) else (dst,)):
                                        stq(dd, ot[o][:], [r_ot[o]], key="ot%d" % o)

                                items.append([s0, s1])
                            first_block = False
                run_pipeline(items, [0, 0, 1, 1, 2, 2])
                fw.barrier()

        with ExitStack() as sA:
            hT = SB(sA, "hT", [128, 16, NT + 1024], BF16)
            wbfA = [SB(sA, "wbf%d" % i, [128, 16, 256], BF16) for i in range(NW)]
            r_wbfA = [fw.res() for _ in range(NW)]
            units_o = [("u", i, 4096 + 256 * i) for i in range(4)] + \
                      [("k", i, 1024 + 256 * i) for i in range(4)] + \
                      [("v", i, 2048 + 256 * i) for i in range(4)]
            prefetch_units(units_o, wbfA, r_wbfA)
            phase_norm(x_ext, hT, 16, col0=1024)
            phase_proj(hT, units_o, rope_oth, False, wbfA, r_wbfA)
            units_w = [("q", i, 0 + 256 * i) for i in range(4)] + \
                      [("k", i, 1024 + 256 * i) for i in range(4)] + \
                      [("v", i, 2048 + 256 * i) for i in range(4)] + \
                      [("ga", i, 3072 + 256 * i) for i in range(4)] + \
                      [("u", i, 4096 + 256 * i) for i in range(4)] + \
                      [("gf", i, 5120 + 256 * i) for i in range(4)]
            prefetch_units(units_w, wbfA, r_wbfA)
            phase_norm(x_own, hT, 16)
            phase_proj(hT, units_w, rope_own, True, wbfA, r_wbfA)

        with ExitStack() as sM:
            mixT = SB(sM, "mixT", [128, 16, NT], BF16)
            r_mix = fw.res("mixT")
            TB0 = SB(sM, "TB0", [128, 16, 2, 256], BF16)
            r_TB0 = [fw.res() for _ in range(2)]

            def prefetch_TB0():
                for cs in range(2):
                    ld(TB0[:, :, cs, :], dft_d[0, 0][:, :, cs, :], [r_TB0[cs]], key="TB0%d" % cs)

            accN7 = SB(sM, "accN7", [128, NT], F32)
            accD7 = SB(sM, "accD7", [128, NT], F32)
            gaT0 = SB(sM, "gaT0", [128, NT], BF16)
            rsmO = [SB(sM, "rsmO%d" % i, [128, 16], F32) for i in range(2)]
            late_fin = []

            with ExitStack() as s2:
                qT = [SB(s2, "qT%d" % i, [128, NT], BF16) for i in range(2)]
                kT = [SB(s2, "kT%d" % i, [128, 4096], BF16) for i in range(2)]
                vT = [SB(s2, "vT%d" % i, [128, 4096], BF16) for i in range(2)]
                gaT = [gaT0]
                r_q = [fw.res() for _ in range(2)]
                r_k = [fw.res() for _ in range(2)]
                r_v = [fw.res() for _ in range(2)]
                r_ga = [fw.res() for _ in range(1)]
                rsm = rsmO
                r_rsm = [fw.res() for _ in range(2)]
                q4 = SB(s2, "q4", [128, NT], BF16)
                q16 = SB(s2, "q16", [128, NT], BF16)
                r_q4 = fw.res()
                r_q16 = fw.res()
                Vd = [SB(s2, "Vd%d" % i, [128, 69, 128], BF16) for i in range(2)]
                r_Vd = [fw.res() for _ in range(2)]
                NP = 6
                PT = [SB(s2, "PT%d" % i, [128, 512], BF16) for i in range(NP)]
                r_PT = [fw.res() for _ in range(NP)]
                accN_ = [SB(s2, "accN0", [128, NT], F32), accN7]
                accD_ = [SB(s2, "accD0", [128, NT], F32), accD7]
                r_accN_ = [fw.res() for _ in range(2)]
                r_accD_ = [fw.res() for _ in range(2)]
                mk = SB(s2, "mk", [128, 2, 256], BF16)
                r_mk = fw.res()
                ld(mk[:], masks_d, [r_mk], key="mk")
                psc = [PS(s2, "psc%d" % i, [128, 512], F32) for i in range(2)]
                r_psc = [fw.res() for _ in range(2)]
                pnum = [PS(s2, "pnum%d" % i, [128, 512], F32) for i in range(2)]
                r_pnum = [fw.res() for _ in range(2)]
                pden = [PS(s2, "pden%d" % i, [128, 512], F32) for i in range(2)]
                r_pden = [fw.res() for _ in range(2)]
                pvt = [PS(s2, "pvt%d" % i, [128, 8, 128], BF16) for i in range(2)]
                r_pvt = [fw.res() for _ in range(2)]
                cnt = {"sc": 0, "pt": 0, "vt": 0, "grp": 0}

                def load_head(h):
                    sl = h % 2
                    ld(vT[sl][:], VS[h], [r_v[sl]], key="hv%d" % sl)
                    ld(kT[sl][:], KS[h], [r_k[sl]], key="hk%d" % sl)
                    ld(qT[sl][:], QS[h], [r_q[sl]], key="hq%d" % sl)

                def load_ga(h):
                    ld(gaT[0][:], GAS[h], [r_ga[0]], key="hg0")

                vbase = {1: 0, 4: 17, 16: 37}

                def vidx(d, r, i):
                    nq = 16 // d
                    return vbase[d] + r * (nq + 1) + i

                def emit_vtrans(h):
                    sl = h % 2
                    tiles = []
                    for d in (1, 4, 16):
                        nq = 16 // d
                        for r in range(d):
                            for i in range(nq + 1):
                                t0 = 1024 // d - 64 + 128 * i
                                tiles.append((vidx(d, r, i), r + d * t0, d))
                    for g0 in range(0, len(tiles), 8):
                        grp = tiles[g0:g0 + 8]
                        b = cnt["vt"] % 2
                        cnt["vt"] += 1
                        for j, (vi, s0, d) in enumerate(grp):
                            tr(pvt[b][:, j, :], vT[sl][:, s0:s0 + 127 * d + 1:d], ident[:],
                               [r_v[sl], r_const], [r_pvt[b]], inc=(j == len(grp) - 1))
                        v0 = grp[0][0]
                        cp(act if (g0 // 8) % 2 == 0 else dve, Vd[sl][:, v0:v0 + len(grp), :],
                           pvt[b][:, 0:len(grp), :], [r_pvt[b]], [r_Vd[sl]])

                items = []
                load_head(0)
                load_ga(0)
                for h in range(8):
                    sl = h % 2
                    gbank = {}
                    tiles = []
                    for d in (1, 4, 16):
                        nq = 16 // d
                        for r in range(d):
                            for i in range(nq + 1):
                                t0 = 1024 // d - 64 + 128 * i
                                jlo, jhi = max(i - 1, 0), min(i, nq - 1)
                                if i == 0:
                                    msk = mk[:, 1, 128:256]
                                elif i == nq:
                                    msk = mk[:, 1, 0:128]
                                else:
                                    msk = mk[:, 0, :]
                                pv = []
                                for j in range(jlo, jhi + 1):
                                    if d == 16:
                                        gkey, col = (d, r // 4, 0), 128 * (r % 4)
                                    else:
                                        gkey, col = (d, r, j // 4), 128 * (j % 4)
                                    if gkey not in gbank:
                                        gbank[gkey] = cnt["grp"] % 2
                                        cnt["grp"] += 1
                                    pv.append((j, gbank[gkey], col))
                                tiles.append(dict(d=d, r=r, i=i, nq=nq, ks0=r + d * t0, jlo=jlo,
                                                  ncol=128 * (jhi - jlo + 1), q0=r + d * 128 * jlo, msk=msk, pv=pv))
                    npairs = (len(tiles) + 1) // 2
                    for pi in range(npairs):
                        pair = tiles[2 * pi:2 * pi + 2]
                        sc = cnt["sc"] % 2
                        cnt["sc"] += 1
                        p = cnt["pt"] % NP
                        cnt["pt"] += 1
                        c0 = 0
                        for tl in pair:
                            tl["c0"] = c0
                            c0 += tl["ncol"]
                        wtot = c0
                        last_pair = (pi == npairs - 1)

                        def s0(h=h, sl=sl, pi=pi, sc=sc, pair=pair):
                            if pi == 0 and h == 0:
                                emit_vtrans(0)
                            if pi == 3 and h + 1 < 8:
                                load_head(h + 1)
                            if pi == 15 and h + 1 < 8:
                                emit_vtrans(h + 1)
                            if pi == 20 and h >= 1:
                                load_ga(h)
                            if pi == 0:
                                cp(pool, q4[:].rearrange("p (r m) -> p r m", r=4),
                                   qT[sl][:].rearrange("p (m r) -> p r m", r=4), [r_q[sl]], [r_q4])
                                cp(pool, q16[:].rearrange("p (r m) -> p r m", r=16),
                                   qT[sl][:].rearrange("p (m r) -> p r m", r=16), [r_q[sl]], [r_q16])
                            if pi == 22 and h == 7:
                                prefetch_TB0()
                            for tl in pair:
                                d, ks0, q0, ncol, c0 = tl["d"], tl["ks0"], tl["q0"], tl["ncol"], tl["c0"]
                                if d == 1:
                                    qmov, r_qm = qT[sl][:, q0:q0 + ncol], r_q[sl]
                                elif d == 4:
                                    qb = 512 * tl["r"] + 128 * tl["jlo"]
                                    qmov, r_qm = q4[:, qb:qb + ncol], r_q4
                                else:
                                    qb = 128 * tl["r"]
                                    qmov, r_qm = q16[:, qb:qb + ncol], r_q16
                                mm(psc[sc][:, c0:c0 + ncol], kT[sl][:, ks0:ks0 + 127 * d + 1:d],
                                   qmov, True, False, [r_qm, r_k[sl]], [r_psc[sc]], inc=False)
                                mm(psc[sc][:, c0:c0 + ncol], ident[:], tl["msk"], False, True, [r_mk, r_const], [r_psc[sc]])

                        def s1(sc=sc, p=p, wtot=wtot):
                            actv(PT[p][:, 0:wtot], psc[sc][:, 0:wtot], AF.Exp, [r_psc[sc]], [r_PT[p]], scale=ATT_SCALE)

                        def s2_(h=h, sl=sl, p=p, pair=pair):
                            accN, accD, r_accN, r_accD = accN_[sl], accD_[sl], r_accN_[sl], r_accD_[sl]
                            for tl in pair:
                                d, r, i, jlo, c0 = tl["d"], tl["r"], tl["i"], tl["jlo"], tl["c0"]
                                for (j, bank, col) in tl["pv"]:
                                    pc = c0 + 128 * (j - jlo)
                                    first = (i == j)
                                    last = (i == j + 1)
                                    mm(pnum[bank][:, col:col + 128], Vd[sl][:, vidx(d, r, i), :], PT[p][:, pc:pc + 128],
                                       first, last, [r_Vd[sl], r_PT[p]], [r_pnum[bank]], inc=last)
                                    mm(pden[bank][:, col:col + 128], ones[:], PT[p][:, pc:pc + 128],
                                       first, last, [r_const, r_PT[p]], [r_pden[bank]], inc=last)
                                    if last:
                                        if d == 1 and j % 4 == 3:
                                            g = j // 4
                                            cp(dve, accN[:, 512 * g:512 * g + 512], pnum[bank][:], [r_pnum[bank]], [r_accN])
                                            cp(act, accD[:, 512 * g:512 * g + 512], pden[bank][:], [r_pden[bank]], [r_accD])
                                        elif d == 4 and j == 3:
                                            vN = accN[:].rearrange("p (m r) -> p r m", r=4)[:, r, :]
                                            vD = accD[:].rearrange("p (m r) -> p r m", r=4)[:, r, :]
                                            tt(dve, vN, pnum[bank][:], vN, ALU.add, [r_pnum[bank], r_accN], [r_accN])
                                            tt(dve, vD, pden[bank][:], vD, ALU.add, [r_pden[bank], r_accD], [r_accD])
                                        elif d == 16 and r % 4 == 3:
                                            r0 = r - 3
                                            vN = accN[:].rearrange("p (m r) -> p r m", r=16)[:, r0:r0 + 4, :]
                                            vD = accD[:].rearrange("p (m r) -> p r m", r=16)[:, r0:r0 + 4, :]
                                            pn = pnum[bank][:].rearrange("p (r m) -> p r m", r=4)
                                            pd = pden[bank][:].rearrange("p (r m) -> p r m", r=4)
                                            tt(dve, vN, pn, vN, ALU.add, [r_pnum[bank], r_accN], [r_accN])
                                            tt(dve, vD, pd, vD, ALU.add, [r_pden[bank], r_accD], [r_accD])

                        def f1(h=h, sl=sl):
                            accD, r_accD = accD_[sl], r_accD_[sl]
                            r_d1 = fw.res()
                            fw.dma(sp, DEN1[h:h + 1, :], accD[0:1, :], [r_accD], [r_d1], key="den1")
                            fw.dma(sp, rsm[sl][:], DEN1[h].rearrange("(p j) -> p j", j=16), [r_d1], [r_rsm[sl]], key="den2")

                        def f2(h=h, sl=sl):
                            accD, r_accD = accD_[sl], r_accD_[sl]
                            r_d2 = fw.res()
                            recip(rsm[sl][:], rsm[sl][:], [r_rsm[sl]], [r_rsm[sl]])
                            fw.dma(sp, DEN2[h].rearrange("(p j) -> p j", j=16), rsm[sl][:], [r_rsm[sl]], [r_d2], key="den3")
                            fw.dma(sp, accD[:], DEN2[h:h + 1, :].broadcast_to([128, NT]), [r_d2], [r_accD], key="den4")

                        def f3(h=h, sl=sl):
                            accN, accD, r_accN, r_accD = accN_[sl], accD_[sl], r_accN_[sl], r_accD_[sl]
                            tt(dve, accN[:], accN[:], accD[:], ALU.mult, [r_accN, r_accD], [r_accN])
                            tt(pool, mixT[:, h, :], accN[:], gaT[0][:], ALU.mult, [r_accN, r_ga[0]], [r_mix])

                        if last_pair and h == 7:
                            late_fin.extend([f1, f2, f3])
                            items.append([s0, s1, s2_])
                        else:
                            items.append([s0, s1, s2_] + ([f1, f2, f3] if last_pair else []))
                run_pipeline(items, [0, 0, 2, 4, 10, 16])
                fw.barrier()

            with ExitStack() as sCD:
                wo01 = SB(sCD, "wo01", [128, 16, 512], BF16)
                r_wo = [fw.res() for _ in range(4)]

                def load_wo(g, wt, lbase):
                    for hh in range(2):
                        c0 = 512 * g + 256 * hh
                        l0 = lbase + 256 * hh
                        ldw(wt[:, :, l0:l0 + 256], w_out[:, c0:c0 + 256].rearrange("(c p) n -> p c n", p=128),
                            [r_wo[g]], key="wo%d" % g)

                with ExitStack() as s2:
                    sdb = SB(s2, "sdb", [128, 16, 1024], BF16)
                    r_sdb = [fw.res() for _ in range(8)]
                    TB = [TB0, SB(s2, "TB1", [128, 16, 2, 256], BF16)]
                    r_TB = [r_TB0, [fw.res() for _ in range(2)]]
                    GF = [SB(s2, "GF%d" % i, [128, 8, 512], BF16) for i in range(2)]
                    r_GF = [fw.res() for _ in range(2)]
                    XS = SB(s2, "XS", [128, 8, 2, 256], BF16)
                    r_XS = [fw.res() for _ in range(8)]
                    csm = SB(s2, "csm", [128, 2, 2, 256], BF16)
                    wfb = SB(s2, "wfb", [128, 4, 2, 256], BF16)
                    r_cw = fw.res()
                    px = [PS(s2, "px%d" % i, [128, 2, 256], F32) for i in range(4)]
                    r_px = [fw.res() for _ in range(4)]
                    pf = [PS(s2, "pf%d" % i, [128, 512], F32) for i in range(2)]
                    r_pf = [fw.res() for _ in range(2)]
                    po = [PS(s2, "po%d" % i, [128, 512], F32) for i in range(2)]
                    r_po = [fw.res() for _ in range(2)]
                    cnt = {"x": 0, "f": 0, "o": 0}
                    slices = [(par, kb) for par in range(2) for kb in range(4)]

                    def load_TB(i):
                        par, kb = slices[i]
                        sl = i % 2
                        for cs in range(2):
                            ld(TB[sl][:, :, cs, :], dft_d[par, kb][:, :, cs, :], [r_TB[sl][cs]], key="TB%d%d" % (sl, cs))

                    def load_GF(i):
                        par, kb = slices[i]
                        sl = i % 2
                        ld(GF[sl][:], GFS[:, :, 512 * kb:512 * kb + 512].rearrange("c p k -> p c k"),
                           [r_GF[sl]], key="GF%d" % sl)

                    def load_sdb(par, cc):
                        ld(sdb[:, :, 128 * cc:128 * cc + 128],
                           SDS[par, :, :, 128 * cc:128 * cc + 128].rearrange("t p c -> p t c"),
                           [r_sdb[cc]], key="sdb%d" % cc)

                    load_sdb(0, 0)
                    ld(csm[:], csm_d, [r_cw], key="cw")
                    ldw(wfb[:], w_f.rearrange("g (c p) e -> p g c e", p=128), [r_cw], key="cw2")
                    for cc in range(1, 8):
                        load_sdb(0, cc)
                    load_GF(0)
                    load_wo(0, wo01, 0)
                    Gm = SB(s2, "Gm", [128, 4, 2, 2, 256], BF16)
                    r_Gm = fw.res()

                    gm_pieces = [(g, cs, ck) for g in range(4) for cs in range(2) for ck in range(2)]

                    def build_Gm(lo, hi):
                        for (g, cs, ck) in gm_pieces[lo:hi]:
                            b = cnt["f"] % 2
                            cnt["f"] += 1
                            for ek in range(2):
                                mm(pf[b][:, 0:256], csm[:, cs, ek, 128 * ck:128 * ck + 128], wfb[:, g, ek, :],
                                   ek == 0, ek == 1, [r_cw], [r_pf[b]], inc=(ek == 1))
                            cp(act, Gm[:, g, cs, ck, :], pf[b][:, 0:256], [r_pf[b]], [r_Gm])

                    for i, (par, kb) in enumerate(slices):
                        sl = i % 2
                        if i < 3:
                            late_fin[i]()
                        if i + 1 < len(slices):
                            load_TB(i + 1)
                            load_GF(i + 1)
                        for cc in range(8):
                            b = cnt["x"] % 4
                            cnt["x"] += 1
                            for cs in range(2):
                                for t in range(16):
                                    mm(px[b][:, cs, :], sdb[:, t, 128 * cc:128 * cc + 128], TB[sl][:, t, cs, :],
                                       t == 0, t == 15, [r_sdb[cc], r_TB[sl][cs]], [r_px[b]], inc=(t == 15 and cs == 1))
                            if par == 0 and kb == 3:
                                load_sdb(1, cc)
                            if i == 0 and cc >= 2:
                                build_Gm(3 * (cc - 2), min(16, 3 * (cc - 2) + 3))
                            cp(act if cc % 2 == 0 else dve, XS[:, cc, :, :], px[b][:], [r_px[b]], [r_XS[cc]])
                        for g in range(4):
                            for oc in range(2):
                                b = cnt["o"] % 2
                                cnt["o"] += 1
                                n = 0
                                for ck in range(2):
                                    for cs in range(2):
                                        mm(po[b][:, 0:256], Gm[:, g, cs, ck, 128 * oc:128 * oc + 128], XS[:, 2 * g + ck, cs, :],
                                           n == 0, n == 3, [r_Gm, r_XS[2 * g + ck]], [r_po[b]], inc=(n == 3))
                                        n += 1
                                ch = 2 * g + oc
                                k0 = 512 * kb + par
                                tt(dve, mixT[:, 8 + ch, k0:512 * kb + 512:2], po[b][:, 0:256], GF[sl][:, ch, par:512:2],
                                   ALU.mult, [r_po[b], r_GF[sl]], [r_mix])
                    fw.barrier()

                if debug:
                    for c in range(16):
                        stq(MIXS[c], mixT[:, c, :], [r_mix], key="dbg")
                    fw.barrier()

                with ExitStack() as s2:
                    wo123 = SB(s2, "wo123", [128, 16, 1536], BF16)
                    for g in range(1, 4):
                        load_wo(g, wo123, 512 * (g - 1))
                    NX = 4
                    xr = [SB(s2, "xr%d" % i, [128, 512], F32) for i in range(NX)]
                    r_xr = [fw.res() for _ in range(NX)]
                    yo = [SB(s2, "yo%d" % i, [128, 512], F32) for i in range(NX)]
                    r_yo = [fw.res() for _ in range(NX)]
                    py = [PS(s2, "py%d" % i, [128, 512], F32) for i in range(4)]
                    r_py = [fw.res() for _ in range(4)]
                    items = []
                    n = 0
                    for g in range(4):
                        wt = wo01 if g == 0 else wo123
                        l0 = 0 if g == 0 else 512 * (g - 1)
                        for t in range(16):
                            xi = n % NX
                            b = n % 4
                            n += 1
                            rows = slice(128 * t, 128 * t + 128)
                            cols = slice(512 * g, 512 * g + 512)

                            def s0(t=t, g=g, xi=xi, b=b, wt=wt, l0=l0, rows=rows, cols=cols):
                                ld(xr[xi][:], x_own[rows, cols], [r_xr[xi]], key="xr%d" % xi)
                                for c in range(16):
                                    mm(py[b][:], mixT[:, c, 128 * t:128 * t + 128], wt[:, c, l0:l0 + 512],
                                       c == 0, c == 15, [r_wo[g], r_mix], [r_py[b]], inc=(c == 15))

                            def s1(xi=xi, b=b, rows=rows, cols=cols):
                                tt(dve, yo[xi][:], py[b][:], xr[xi][:], ALU.add, [r_py[b], r_xr[xi]], [r_yo[xi]])
                                stq(y[rows, cols], yo[xi][:], [r_yo[xi]], key="yo%d" % xi)

                            items.append([s0, s1])
                    run_pipeline(items, [0, 1])
                    fw.barrier()
        fw.flush()
    return nc


def _rope_tables(pos):
    half = 16
    inv_freq = 1.0 / (500000.0 ** (np.arange(half, dtype=np.float64) / half))
    ang = pos.astype(np.float64)[None, :] * inv_freq[:, None]
    cos = np.cos(ang)
    sin = np.sin(ang)
    cosT = np.concatenate([cos, cos], axis=0)
    sinT = np.concatenate([-sin, sin], axis=0)
    return np.stack([cosT, sinT]).astype(np.float32)


def _dft_tables(kind):
    n = np.arange(2048, dtype=np.int64)
    j = np.arange(1024, dtype=np.int64)
    out = np.empty((2, 2, 2048, 1024), dtype=np.float32)
    for par in range(2):
        if kind == "A":
            N, k, sgn = 4096, 2 * j + par, 1.0
        elif kind == "B":
            N, k, sgn = 4096, 2048 + 2 * j + par, (1.0 if par == 0 else -1.0)
        else:
            N, k, sgn = 2048, 2 * j + par, 1.0
        ph = (n[:, None] * k[None, :]) % N
        th = 2.0 * np.pi * ph.astype(np.float64) / N
        out[par, 0] = sgn * np.cos(th)
        out[par, 1] = -sgn * np.sin(th)
    o = out.reshape(2, 2, 16, 128, 4, 256).transpose(0, 4, 3, 2, 1, 5)
    return np.ascontiguousarray(o).astype(BF)


def _csm_table(S):
    scale = 1.0 / math.sqrt(S * 256.0)
    c = np.arange(256, dtype=np.int64)
    ph = (c[:, None] * c[None, :]) % 256
    th = 2.0 * np.pi * ph.astype(np.float64) / 256
    m = np.stack([np.cos(th), np.sin(th)]) * scale
    o = m.reshape(2, 2, 128, 256).transpose(2, 0, 1, 3)
    return np.ascontiguousarray(o).astype(BF)


def _masks(kind):
    a = np.arange(128)[:, None]
    b = np.arange(256)[None, :]
    band = ((b - a) >= 0) & ((b - a) <= 128)
    left_valid = kind == "B"
    right_valid = kind == "A"
    edge = band.copy()
    if not left_valid:
        edge[:64, 128:256] = False
    if not right_valid:
        edge[64:, 0:128] = False
    m = np.stack([band, edge], axis=1).astype(np.float32)
    return ((m - 1.0) * 30000.0).astype(BF)


_NC_CACHE = {}
_CONST_CACHE = {}


def _consts(kind):
    if kind not in _CONST_CACHE:
        S = 4096 if kind in ("A", "B") else 2048
        _CONST_CACHE[kind] = dict(dft=_dft_tables(kind), csm=_csm_table(S), masks=_masks(kind))
    return _CONST_CACHE[kind]


def make_in_maps(x_prompt, x_sample, rms_gain, w_in, q_norm_gain, k_norm_gain, w_fourier, w_out):
    ident = np.eye(128, dtype=np.float32).astype(BF)
    ones = np.ones((128, 128), dtype=np.float32).astype(BF)
    psw = np.zeros((128, 128), dtype=np.float32)
    for m in range(16):
        psw[m + 16, m] = 1.0
        psw[m, m + 16] = 1.0
    psw = psw.astype(BF)
    shared = dict(
        rms_g=np.ascontiguousarray(rms_gain[0].reshape(1, DM)),
        w_in=np.ascontiguousarray(w_in[0]),
        qg=np.ascontiguousarray(q_norm_gain[0].reshape(128, 1)),
        kg=np.ascontiguousarray(k_norm_gain[0].reshape(128, 1)),
        w_f=np.ascontiguousarray(w_fourier[0]),
        w_out=np.ascontiguousarray(w_out[0]),
        ident=ident, ones=ones, pswap=psw,
    )
    zeros = np.zeros((NT, DM), dtype=np.float32)
    in_maps = []
    for c in range(8):
        if c < 4:
            b, half = c // 2, c % 2
            kind = "A" if half == 0 else "B"
            xo = x_prompt[b, 2048 * half:2048 * half + 2048]
            xt = x_prompt[b, 2048 * (1 - half):2048 * (1 - half) + 2048]
            h0 = 2048 if half == 0 else 1024
            xe = xt
            hoff = 1024 + (h0 - 2048 * (1 - half))
            pos_own = 2048 * half + np.arange(NT)
            pos_oth = np.concatenate([h0 + np.arange(1024), np.zeros(1024, dtype=np.int64)])
        else:
            kind = "S"
            xo = x_sample[c - 4]
            xe = zeros
            hoff = 1024
            pos_own = np.arange(NT)
            pos_oth = np.zeros(NT, dtype=np.int64)
        cst = _consts(kind)
        m = dict(shared)
        m.update(
            x_own=np.ascontiguousarray(xo), x_ext=np.ascontiguousarray(xe),
            hoff=np.array([[hoff]], dtype=np.int32),
            rope_own=_rope_tables(pos_own), rope_oth=_rope_tables(pos_oth),
            masks=cst["masks"], dft=cst["dft"], csm=cst["csm"],
        )
        in_maps.append(m)
    return in_maps


def kernel(x_prompt, x_sample, rms_gain, w_in, q_norm_gain, k_norm_gain, w_fourier, w_out):
    args = [np.asarray(a) for a in (x_prompt, x_sample, rms_gain, w_in, q_norm_gain, k_norm_gain, w_fourier, w_out)]
    in_maps = make_in_maps(*args)
    if "nc" not in _NC_CACHE:
        _NC_CACHE["nc"] = build()
    res = run_bass_kernel_spmd(_NC_CACHE["nc"], in_maps, core_ids=list(range(8)))
    outs = [np.asarray(r["y"], dtype=np.float32) for r in res.results]
    y_prompt = np.stack([np.concatenate([outs[0], outs[1]], axis=0),
                         np.concatenate([outs[2], outs[3]], axis=0)], axis=0)
    y_sample = np.stack(outs[4:8], axis=0)
    return (y_prompt, y_sample)
```

```python
import math
from contextlib import ExitStack

import numpy as np
import ml_dtypes

import concourse.bass as bass
import concourse.mybir as mybir
from concourse.bass_utils import run_bass_kernel_spmd

F32 = mybir.dt.float32
BF16 = mybir.dt.bfloat16
AF = mybir.ActivationFunctionType
ALU = mybir.AluOpType

NT = 2048
DM = 2048
EPS = 1e-6
ATT_SCALE = 128 ** -0.5
BF = ml_dtypes.bfloat16


class Ev:
    __slots__ = ("sem", "val", "sid")

    def __init__(self, sem, val, sid):
        self.sem = sem
        self.val = val
        self.sid = sid


class Res:
    __slots__ = ("name", "w", "r")

    def __init__(self, name):
        self.name = name
        self.w = None
        self.r = {}


class Eng:
    def __init__(self, fw, name, is_pe=False):
        self.name = name
        self.is_pe = is_pe
        self.sem = fw.new_sem("e_" + name)
        self.sid = fw.sid(self.sem)
        self.count = 0
        self.seen = {}
        self.prog = []
        self.nins = 0


class FW:
    def __init__(self, nc, stack):
        self.nc = nc
        self.stack = stack
        self._sids = {}
        self.nsem = 0
        self.pe = Eng(self, "tensor", is_pe=True)
        self.act = Eng(self, "scalar")
        self.dve = Eng(self, "vector")
        self.pool = Eng(self, "gpsimd")
        self.sp = Eng(self, "sync")
        self.engs = [self.pe, self.act, self.dve, self.pool, self.sp]
        self.dsems = {}
        self.dcount = {}

    def new_sem(self, name):
        self.nsem += 1
        return self.stack.enter_context(self.nc.semaphore(name))

    def sid(self, sem):
        k = id(sem)
        if k not in self._sids:
            self._sids[k] = len(self._sids)
        return self._sids[k]

    def res(self, name="r"):
        return Res(name)

    def _wait(self, eng, ev):
        if ev is None:
            return
        if eng.is_pe and ev.sid == eng.sid:
            return
        if eng.seen.get(ev.sid, 0) >= ev.val:
            return
        eng.seen[ev.sid] = ev.val
        sem, val = ev.sem, ev.val
        eng.prog.append(lambda h: h.wait_ge(sem, val))

    def _deps(self, eng, reads, writes):
        for r in reads:
            self._wait(eng, r.w)
        for w in writes:
            self._wait(eng, w.w)
            for e in w.r.values():
                self._wait(eng, e)

    def _commit(self, ev, reads, writes):
        for r in reads:
            old = r.r.get(ev.sid)
            if old is None or old.val < ev.val:
                r.r[ev.sid] = ev
        for w in writes:
            w.w = ev
            w.r = {}

    def op(self, eng, fn, reads=(), writes=(), inc=True):
        self._deps(eng, reads, writes)
        eng.nins += 1
        if inc:
            eng.count += 1
            ev = Ev(eng.sem, eng.count, eng.sid)
            sem = eng.sem
            eng.prog.append(lambda h: fn(h).then_inc(sem, 1))
        else:
            ev = Ev(eng.sem, eng.count + 1, eng.sid)
            eng.prog.append(lambda h: fn(h))
        self._commit(ev, reads, writes)
        return ev

    def dma(self, q, out, in_, reads=(), writes=(), key=None, **kw):
        self._deps(q, reads, writes)
        if key not in self.dsems:
            self.dsems[key] = self.new_sem("d_" + key)
            self.dcount[key] = 0
        sem = self.dsems[key]
        self.dcount[key] += 16
        ev = Ev(sem, self.dcount[key], self.sid(sem))
        q.nins += 1
        q.prog.append(lambda h: h.dma_start(out=out, in_=in_, **kw).then_inc(sem, 16))
        self._commit(ev, reads, writes)
        return ev

    def dma_fn(self, q, fn, reads=(), writes=(), key=None):
        self._deps(q, reads, writes)
        if key not in self.dsems:
            self.dsems[key] = self.new_sem("d_" + key)
            self.dcount[key] = 0
        sem = self.dsems[key]
        self.dcount[key] += 16
        ev = Ev(sem, self.dcount[key], self.sid(sem))
        q.nins += 1
        q.prog.append(lambda h: fn(h).then_inc(sem, 16))
        self._commit(ev, reads, writes)
        return ev

    def barrier(self):
        evs = []
        for key, sem in self.dsems.items():
            evs.append(Ev(sem, self.dcount[key], self.sid(sem)))
        for e in self.engs:
            if e.count > 0:
                evs.append(Ev(e.sem, e.count, e.sid))
        for e in self.engs:
            for ev in evs:
                if ev.sid != e.sid:
                    self._wait(e, ev)

    def flush(self):
        nc = self.nc
        for key, sem in self.dsems.items():
            self._wait(self.sp, Ev(sem, self.dcount[key], self.sid(sem)))
        for e in self.engs:
            if e is not self.sp and e.count > 0:
                self._wait(self.sp, Ev(e.sem, e.count, e.sid))
        progs = {e.name: e.prog for e in self.engs}
        for e in self.engs:
            e.prog = []

        def run(name):
            def body(h):
                for c in progs[name]:
                    c(h)
            return body

        with nc.Block(no_gpsimd_drain=True) as block:
            block.tensor(run("tensor"))
            block.scalar(run("scalar"))
            block.vector(run("vector"))
            block.gpsimd(run("gpsimd"))
            block.sync(run("sync"))


def build(debug=False):
    nc = bass.Bass("TRN2", target_bir_lowering=False)

    def D(name, shape, dt, kind="ExternalInput"):
        return nc.dram_tensor(name, shape, dt, kind=kind).ap()

    x_own = D("x_own", [NT, DM], F32)
    x_ext = D("x_ext", [NT, DM], F32)
    hoff_d = D("hoff", [1, 1], mybir.dt.int32)
    rms_g = D("rms_g", [1, DM], F32)
    w_in = D("w_in", [DM, 6144], F32)
    qg = D("qg", [128, 1], F32)
    kg = D("kg", [128, 1], F32)
    w_f = D("w_f", [4, 256, 256], F32)
    w_out = D("w_out", [DM, DM], F32)
    ident_d = D("ident", [128, 128], BF16)
    ones_d = D("ones", [128, 128], BF16)
    pswap_d = D("pswap", [128, 128], BF16)
    rope_own = D("rope_own", [2, 32, NT], F32)
    rope_oth = D("rope_oth", [2, 32, NT], F32)
    masks_d = D("masks", [128, 2, 256], BF16)
    dft_d = D("dft", [2, 4, 128, 16, 2, 256], BF16)
    csm_d = D("csm", [128, 2, 2, 256], BF16)
    y = D("y", [NT, DM], F32, kind="ExternalOutput")

    sk = "ExternalOutput" if debug else "Internal"
    KS = D("KS", [8, 128, 4096], BF16, kind=sk)
    VS = D("VS", [8, 128, 4096], BF16, kind=sk)
    QS = D("QS", [8, 128, NT], BF16, kind=sk)
    GAS = D("GAS", [8, 128, NT], BF16, kind=sk)
    GFS = D("GFS", [8, 128, NT], BF16, kind=sk)
    UOS = D("UOS", [16, 128, 1024], BF16, kind=sk)
    SDS = D("SDS", [2, 16, 128, 1024], BF16, kind=sk)
    DEN1 = D("DEN1", [8, 2048], F32, kind=sk)
    DEN2 = D("DEN2", [8, 2048], F32, kind=sk)
    MIXS = D("MIXS", [16, 128, NT], BF16, kind="ExternalOutput") if debug else None

    with ExitStack() as st:
        fw = FW(nc, st)
        pe, act, dve, pool, sp = fw.pe, fw.act, fw.dve, fw.pool, fw.sp

        def mm(out, lhsT, rhs, start, stop, R=(), W=(), inc=True):
            return fw.op(pe, lambda h: h.matmul(out, lhsT=lhsT, rhs=rhs, start=start, stop=stop), R, W, inc)

        def tr(out, in_, idn, R=(), W=(), inc=True):
            return fw.op(pe, lambda h: h.transpose(out=out, in_=in_, identity=idn), R, W, inc)

        def actv(out, in_, func, R=(), W=(), scale=None, bias=None, accum=None):
            kw = {}
            if scale is not None:
                kw["scale"] = scale
            if bias is not None:
                kw["bias"] = bias
            if accum is not None:
                kw["accum_out"] = accum
            return fw.op(act, lambda h: h.activation(out=out, in_=in_, func=func, **kw), R, W)

        def tt(eng, out, a, b, op, R=(), W=()):
            return fw.op(eng, lambda h: h.tensor_tensor(out=out, in0=a, in1=b, op=op), R, W)

        def stt(out, in0, scalar, in1, op0, op1, R=(), W=()):
            return fw.op(dve, lambda h: h.scalar_tensor_tensor(out=out, in0=in0, scalar=scalar, in1=in1,
                                                               op0=op0, op1=op1), R, W)

        def cp(eng, out, in_, R=(), W=()):
            if eng is act:
                return fw.op(act, lambda h: h.copy(out=out, in_=in_), R, W)
            return fw.op(eng, lambda h: h.tensor_copy(out=out, in_=in_), R, W)

        def recip(out, in_, R=(), W=()):
            return fw.op(dve, lambda h: h.reciprocal(out=out, in_=in_), R, W)

        def ld(out, in_, W=(), key=None, **kw):
            return fw.dma(sp, out, in_, (), W, key=key, **kw)

        def stq(out, in_, R=(), key=None, **kw):
            return fw.dma(sp, out, in_, R, (), key=key, **kw)

        uniq = [0]

        def SB(stack, name, shape, dt):
            uniq[0] += 1
            return stack.enter_context(nc.sbuf_tensor("sb%d_%s" % (uniq[0], name), shape, dt))

        def PS(stack, name, shape, dt):
            uniq[0] += 1
            return stack.enter_context(nc.psum_tensor("ps%d_%s" % (uniq[0], name), shape, dt))

        ident = SB(st, "ident", [128, 128], BF16)
        ones = SB(st, "ones", [128, 128], BF16)
        pswap = SB(st, "pswap", [128, 128], BF16)
        qg_t = SB(st, "qg_t", [128, 1], F32)
        kg_t = SB(st, "kg_t", [128, 1], F32)
        r_const = fw.res("const")
        ld(ident[:], ident_d, [r_const], key="c0")
        ld(ones[:], ones_d, [r_const], key="c0")
        ld(pswap[:], pswap_d, [r_const], key="c0")
        ld(qg_t[:], qg, [r_const], key="c0")
        ld(kg_t[:], kg, [r_const], key="c0")

        def run_pipeline(items, skew):
            n = len(items)
            ns = len(skew)
            for step in range(n + max(skew)):
                for s in range(ns):
                    b = step - skew[s]
                    if 0 <= b < n and s < len(items[b]) and items[b][s] is not None:
                        items[b][s]()

        def ldw(out, in_, W=(), key=None):
            return fw.dma(pool, out, in_, (), W, key=key)

        def phase_norm(x_src, hT, ntiles, col0=0):
            with ExitStack() as s2:
                NXT = 3
                xt = [SB(s2, "xt%d" % i, [128, DM], F32) for i in range(NXT)]
                r_xt = [fw.res() for _ in range(NXT)]
                junk = SB(s2, "junk", [128, DM], BF16)
                r_junk = fw.res()
                xn = [SB(s2, "xn%d" % i, [128, DM], BF16) for i in range(NXT)]
                r_xn = [fw.res() for _ in range(NXT)]
                gb = SB(s2, "gb", [128, DM], F32)
                r_gb = fw.res()
                ss = [SB(s2, "ss%d" % i, [128, 1], F32) for i in range(NXT)]
                r_ss = [fw.res() for _ in range(NXT)]
                ptb = [PS(s2, "ptb%d" % i, [128, 8, 128], BF16) for i in range(6)]
                r_ptb = [fw.res() for _ in range(6)]
                ld(gb[:], rms_g.broadcast_to([128, DM]), [r_gb], key="gb")
                items = []
                for t in range(ntiles):
                    sl = t % NXT

                    def s0(t=t, sl=sl):
                        ld(xt[sl][:], x_src[128 * t:128 * t + 128, :], [r_xt[sl]], key="xt%d" % sl)

                    def s1(t=t, sl=sl):
                        actv(junk[:], xt[sl][:], AF.Square, [r_xt[sl]], [r_junk, r_ss[sl]], accum=ss[sl][:])
                        actv(ss[sl][:], ss[sl][:], AF.Sqrt, [r_ss[sl]], [r_ss[sl]], scale=1.0 / DM, bias=EPS)
                        recip(ss[sl][:], ss[sl][:], [r_ss[sl]], [r_ss[sl]])
                        stt(xn[sl][:], xt[sl][:], ss[sl][:], gb[:], ALU.mult, ALU.mult,
                            [r_xt[sl], r_ss[sl], r_gb], [r_xn[sl]])

                    def s2_(t=t, sl=sl):
                        for half in range(2):
                            bi = (2 * t + half) % 6
                            for c in range(8):
                                cc = 8 * half + c
                                tr(ptb[bi][:, c, :], xn[sl][:, cc * 128:(cc + 1) * 128], ident[:],
                                   [r_xn[sl], r_const], [r_ptb[bi]], inc=(c == 7))

                    def s3(t=t, sl=sl):
                        for half in range(2):
                            bi = (2 * t + half) % 6
                            cp(act if half == 0 else dve, hT[:, 8 * half:8 * half + 8, col0 + 128 * t:col0 + 128 * t + 128],
                               ptb[bi][:], [r_ptb[bi]], [])

                    items.append([s0, s1, s2_, s3])
                run_pipeline(items, [0, 1, 2, 3])
                fw.barrier()

        NW = 4

        NPRE = 1

        def prefetch_units(units, wbf, r_wbf):
            for i in range(min(NPRE, len(units))):
                kind, idx, col0 = units[i]
                ldw(wbf[i % NW][:], w_in[:, col0:col0 + 256].rearrange("(c p) n -> p c n", p=128),
                    [r_wbf[i % NW]], key="wbf%d" % (i % NW))

        def phase_proj(hT, units, rope_d, own, wbf, r_wbf):
            with ExitStack() as s2:
                cosT = SB(s2, "cosT", [32, NT], F32)
                sinT = SB(s2, "sinT", [32, NT], F32)
                r_rope = fw.res()
                ld(cosT[:], rope_d[0], [r_rope], key="rope")
                ld(sinT[:], rope_d[1], [r_rope], key="rope")
                NS = 3
                NO = 5
                sq = [SB(s2, "sq%d" % i, [128, 512], BF16) for i in range(NS)]
                r_sq = [fw.res() for _ in range(NS)]
                rs = [SB(s2, "rs%d" % i, [128, 512], F32) for i in range(NS)]
                r_rs = [fw.res() for _ in range(NS)]
                ot = [SB(s2, "ot%d" % i, [128, 512], BF16) for i in range(NO)]
                r_ot = [fw.res() for _ in range(NO)]
                t1 = [SB(s2, "t1_%d" % i, [32, 512], F32) for i in range(NS)]
                r_t1 = [fw.res() for _ in range(NS)]
                t2 = [SB(s2, "t2_%d" % i, [32, 512], F32) for i in range(NS)]
                r_t2 = [fw.res() for _ in range(NS)]
                ub = [SB(s2, "ub%d" % i, [128, 16, 256], BF16) for i in range(2)]
                r_ub = [[fw.res() for _ in range(2)] for _ in range(2)]
                uo2 = [SB(s2, "uo%d" % i, [128, 16, 256], BF16) for i in range(2)]
                r_uo2 = [fw.res() for _ in range(2)]
                tok0 = 0 if own else 1024

                def load_uo(j):
                    kind_, idx_, _c = units[j]
                    ld(uo2[idx_ % 2][:], UOS[:, :, 256 * idx_:256 * idx_ + 256].rearrange("t p c -> p t c"),
                       [r_uo2[idx_ % 2]], key="uo%d" % (idx_ % 2))
                NA = 4
                pacc = [PS(s2, "pacc%d" % i, [128, 512], F32) for i in range(NA)]
                r_pacc = [fw.res() for _ in range(NA)]
                pss = [PS(s2, "pss%d" % i, [128, 512], F32) for i in range(2)]
                r_pss = [fw.res() for _ in range(2)]
                ppq = [PS(s2, "ppq%d" % i, [128, 512], F32) for i in range(2)]
                r_ppq = [fw.res() for _ in range(2)]
                cnt = {"acc": 0, "pp": 0, "ot": 0}

                def load_unit(i):
                    kind, idx, col0 = units[i]
                    sl = i % NW
                    ldw(wbf[sl][:], w_in[:, col0:col0 + 256].rearrange("(c p) n -> p c n", p=128),
                        [r_wbf[sl]], key="wbf%d" % sl)

                for i in range(NPRE, min(NW - 1, len(units))):
                    load_unit(i)
                r_halo = fw.res()
                if not own:
                    state = {}

                    def halo_dma(h, c4):
                        if "c" not in state:
                            reg = h.alloc_register("hoff")
                            h.reg_load(reg, hoff_d[0:1, 0:1])
                            state["c"] = h.snap(reg, min_val=1024, max_val=2048)
                        return h.dma_start(out=hT[:, 4 * c4:4 * c4 + 4, 0:1024],
                                           in_=hT[:, 4 * c4:4 * c4 + 4, bass.ds(state["c"], 1024)])

                    def issue_halo():
                        for c4 in range(4):
                            fw.dma_fn(sp, lambda h, c4=c4: halo_dma(h, c4), (), [r_halo], key="halo")
                else:
                    issue_halo = None
                items = []
                for i, (kind, idx, col0) in enumerate(units):
                    sl = i % NW
                    first_block = True
                    if kind == "u":
                        cs = slice(256 * idx, 256 * idx + 256)
                        for t in range(16):
                            a = cnt["acc"] % NA
                            cnt["acc"] += 1

                            def s0(i=i, sl=sl, t=t, a=a, fb=first_block, idx=idx):
                                if fb:
                                    if i == 2 and issue_halo is not None:
                                        issue_halo()
                                    if i + NW - 1 < len(units):
                                        load_unit(i + NW - 1)
                                    if own and i + 1 < len(units) and units[i + 1][0] == "u":
                                        load_uo(i + 1)
                                tk = tok0 + 128 * t
                                for c in range(16):
                                    mm(pacc[a][:, 0:256], hT[:, c, tk:tk + 128], wbf[sl][:, c, :],
                                       c == 0, c == 15, [r_wbf[sl]], [r_pacc[a]], inc=(c == 15))

                            def s1(t=t, a=a, cs=cs, idx=idx):
                                uo, r_uo = uo2[idx % 2], r_uo2[idx % 2]
                                if own:
                                    tt(dve, ub[0][:, t, :], pacc[a][:, 0:256], uo[:, t, :], ALU.add,
                                       [r_pacc[a], r_uo], [r_ub[0][t // 8]])
                                    tt(dve, ub[1][:, t, :], pacc[a][:, 0:256], uo[:, t, :], ALU.subtract,
                                       [r_pacc[a], r_uo], [r_ub[1][t // 8]])
                                else:
                                    cp(act, ub[0][:, t, :], pacc[a][:, 0:256], [r_pacc[a]], [r_ub[0][t // 8]])
                                if t % 8 == 7:
                                    hs = slice(t - 7, t + 1)
                                    hh = t // 8
                                    if own:
                                        stq(SDS[0, hs, :, cs].rearrange("t p c -> p t c"), ub[0][:, hs, :], [r_ub[0][hh]], key="ub0%d" % hh)
                                        stq(SDS[1, hs, :, cs].rearrange("t p c -> p t c"), ub[1][:, hs, :], [r_ub[1][hh]], key="ub1%d" % hh)
                                    else:
                                        stq(UOS[hs, :, cs].rearrange("t p c -> p t c"), ub[0][:, hs, :], [r_ub[0][hh]], key="ub0%d" % hh)

                            items.append([s0, s1])
                            first_block = False
                        continue
                    for sub in range(2):
                        head = 2 * idx + sub
                        for b in range(4 if own else 2):
                            a = cnt["acc"] % NA
                            cnt["acc"] += 1
                            o = cnt["ot"] % NO
                            cnt["ot"] += 1
                            p = cnt["pp"] % NS
                            p2 = cnt["pp"] % 2
                            if kind in ("q", "k"):
                                cnt["pp"] += 1
                            tsl = slice(512 * b, 512 * b + 512)
                            if kind == "q":
                                dst = QS[head, :, tsl]
                            elif kind == "ga":
                                dst = GAS[head, :, tsl]
                            elif kind == "gf":
                                dst = GFS[head, :, tsl]
                            else:
                                T = KS if kind == "k" else VS
                                if own:
                                    dst = T[head, :, 1024 + 512 * b:1024 + 512 * b + 512]
                                else:
                                    dst = (T[head, :, 512 * b:512 * b + 512],
                                           T[head, :, 3072 + 512 * b:3072 + 512 * b + 512])
                            gain = qg_t if kind == "q" else kg_t

                            def s0(i=i, sl=sl, sub=sub, b=b, a=a, fb=first_block):
                                if fb and i + NW - 1 < len(units):
                                    load_unit(i + NW - 1)
                                if fb and own and i + 1 < len(units) and units[i + 1][0] == "u":
                                    load_uo(i + 1)
                                for c in range(16):
                                    mm(pacc[a][:], wbf[sl][:, c, 128 * sub:128 * sub + 128],
                                       hT[:, c, 512 * b:512 * b + 512],
                                       c == 0, c == 15, [r_wbf[sl], r_halo], [r_pacc[a]], inc=(c == 15))

                            if kind in ("q", "k"):
                                def s1(a=a, p=p):
                                    actv(sq[p][:], pacc[a][:], AF.Square, [r_pacc[a]], [r_sq[p]])

                                def s2_(p=p, p2=p2):
                                    mm(pss[p2][:], ones[:], sq[p][:], True, True, [r_sq[p], r_const], [r_pss[p2]])

                                def s3(a=a, p=p, p2=p2, o=o, gain=gain):
                                    actv(rs[p][:], pss[p2][:], AF.Ln, [r_pss[p2]], [r_rs[p]], scale=1.0 / 128, bias=EPS)
                                    actv(rs[p][:], rs[p][:], AF.Exp, [r_rs[p]], [r_rs[p]], scale=-0.5)
                                    stt(ot[o][:], pacc[a][:], gain[:], rs[p][:], ALU.mult, ALU.mult,
                                        [r_pacc[a], r_rs[p], r_const], [r_ot[o]])

                                def s4(p2=p2, o=o):
                                    mm(ppq[p2][:], pswap[:], ot[o][:], True, True, [r_ot[o], r_const], [r_ppq[p2]])

                                def s5(p=p, p2=p2, o=o, tsl=tsl, dst=dst):
                                    tt(pool, t1[p][:], ot[o][0:32, :], cosT[:, tsl], ALU.mult,
                                       [r_ot[o], r_rope], [r_t1[p]])
                                    tt(dve, t2[p][:], ppq[p2][0:32, :], sinT[:, tsl], ALU.mult,
                                       [r_ppq[p2], r_rope], [r_t2[p]])
                                    tt(pool, ot[o][0:32, :], t1[p][:], t2[p][:], ALU.add,
                                       [r_t1[p], r_t2[p]], [r_ot[o]])
                                    for dd in (dst if isinstance(dst, tuple) else (dst,)):
                                        stq(dd, ot[o][:], [r_ot[o]], key="ot%d" % o)

                                items.append([s0, s1, s2_, s3, s4, s5])
                            else:
                                def s1(a=a, o=o, kind=kind, dst=dst):
                                    if kind == "v":
                                        cp(act, ot[o][:], pacc[a][:], [r_pacc[a]], [r_ot[o]])
                                    else:
                                        actv(ot[o][:], pacc[a][:], AF.Silu, [r_pacc[a]], [r_ot[o]])
                                    for dd in (dst if isinstance(dst, tuple) else (dst,)):
                                        stq(dd, ot[o][:], [r_ot[o]], key="ot%d" % o)

                                items.append([s0, s1])
                            first_block = False
                run_pipeline(items, [0, 0, 1, 1, 2, 2])
                fw.barrier()

        with ExitStack() as sA:
            hT = SB(sA, "hT", [128, 16, NT + 1024], BF16)
            wbfA = [SB(sA, "wbf%d" % i, [128, 16, 256], BF16) for i in range(NW)]
            r_wbfA = [fw.res() for _ in range(NW)]
            units_o = [("u", i, 4096 + 256 * i) for i in range(4)] + \
                      [("k", i, 1024 + 256 * i) for i in range(4)] + \
                      [("v", i, 2048 + 256 * i) for i in range(4)]
            prefetch_units(units_o, wbfA, r_wbfA)
            phase_norm(x_ext, hT, 16, col0=1024)
            phase_proj(hT, units_o, rope_oth, False, wbfA, r_wbfA)
            units_w = [("q", i, 0 + 256 * i) for i in range(4)] + \
                      [("k", i, 1024 + 256 * i) for i in range(4)] + \
                      [("v", i, 2048 + 256 * i) for i in range(4)] + \
                      [("ga", i, 3072 + 256 * i) for i in range(4)] + \
                      [("u", i, 4096 + 256 * i) for i in range(4)] + \
                      [("gf", i, 5120 + 256 * i) for i in range(4)]
            prefetch_units(units_w, wbfA, r_wbfA)
            phase_norm(x_own, hT, 16)
            phase_proj(hT, units_w, rope_own, True, wbfA, r_wbfA)

        with ExitStack() as sM:
            mixT = SB(sM, "mixT", [128, 16, NT], BF16)
            r_mix = fw.res("mixT")
            TB0 = SB(sM, "TB0", [128, 16, 2, 256], BF16)
            r_TB0 = [fw.res() for _ in range(2)]

            def prefetch_TB0():
                for cs in range(2):
                    ld(TB0[:, :, cs, :], dft_d[0, 0][:, :, cs, :], [r_TB0[cs]], key="TB0%d" % cs)

            accN7 = SB(sM, "accN7", [128, NT], F32)
            accD7 = SB(sM, "accD7", [128, NT], F32)
            gaT0 = SB(sM, "gaT0", [128, NT], BF16)
            rsmO = [SB(sM, "rsmO%d" % i, [128, 16], F32) for i in range(2)]
            late_fin = []

            with ExitStack() as s2:
                qT = [SB(s2, "qT%d" % i, [128, NT], BF16) for i in range(2)]
                kT = [SB(s2, "kT%d" % i, [128, 4096], BF16) for i in range(2)]
                vT = [SB(s2, "vT%d" % i, [128, 4096], BF16) for i in range(2)]
                gaT = [gaT0]
                r_q = [fw.res() for _ in range(2)]
                r_k = [fw.res() for _ in range(2)]
                r_v = [fw.res() for _ in range(2)]
                r_ga = [fw.res() for _ in range(1)]
                rsm = rsmO
                r_rsm = [fw.res() for _ in range(2)]
                q4 = SB(s2, "q4", [128, NT], BF16)
                q16 = SB(s2, "q16", [128, NT], BF16)
                r_q4 = fw.res()
                r_q16 = fw.res()
                Vd = [SB(s2, "Vd%d" % i, [128, 69, 128], BF16) for i in range(2)]
                r_Vd = [fw.res() for _ in range(2)]
                NP = 6
                PT = [SB(s2, "PT%d" % i, [128, 512], BF16) for i in range(NP)]
                r_PT = [fw.res() for _ in range(NP)]
                accN_ = [SB(s2, "accN0", [128, NT], F32), accN7]
                accD_ = [SB(s2, "accD0", [128, NT], F32), accD7]
                r_accN_ = [fw.res() for _ in range(2)]
                r_accD_ = [fw.res() for _ in range(2)]
                mk = SB(s2, "mk", [128, 2, 256], BF16)
                r_mk = fw.res()
                ld(mk[:], masks_d, [r_mk], key="mk")
                psc = [PS(s2, "psc%d" % i, [128, 512], F32) for i in range(2)]
                r_psc = [fw.res() for _ in range(2)]
                pnum = [PS(s2, "pnum%d" % i, [128, 512], F32) for i in range(2)]
                r_pnum = [fw.res() for _ in range(2)]
                pden = [PS(s2, "pden%d" % i, [128, 512], F32) for i in range(2)]
                r_pden = [fw.res() for _ in range(2)]
                pvt = [PS(s2, "pvt%d" % i, [128, 8, 128], BF16) for i in range(2)]
                r_pvt = [fw.res() for _ in range(2)]
                cnt = {"sc": 0, "pt": 0, "vt": 0, "grp": 0}

                def load_head(h):
                    sl = h % 2
                    ld(vT[sl][:], VS[h], [r_v[sl]], key="hv%d" % sl)
                    ld(kT[sl][:], KS[h], [r_k[sl]], key="hk%d" % sl)
                    ld(qT[sl][:], QS[h], [r_q[sl]], key="hq%d" % sl)

                def load_ga(h):
                    ld(gaT[0][:], GAS[h], [r_ga[0]], key="hg0")

                vbase = {1: 0, 4: 17, 16: 37}

                def vidx(d, r, i):
                    nq = 16 // d
                    return vbase[d] + r * (nq + 1) + i

                def emit_vtrans(h):
                    sl = h % 2
                    tiles = []
                    for d in (1, 4, 16):
                        nq = 16 // d
                        for r in range(d):
                            for i in range(nq + 1):
                                t0 = 1024 // d - 64 + 128 * i
                                tiles.append((vidx(d, r, i), r + d * t0, d))
                    for g0 in range(0, len(tiles), 8):
                        grp = tiles[g0:g0 + 8]
                        b = cnt["vt"] % 2
                        cnt["vt"] += 1
                        for j, (vi, s0, d) in enumerate(grp):
                            tr(pvt[b][:, j, :], vT[sl][:, s0:s0 + 127 * d + 1:d], ident[:],
                               [r_v[sl], r_const], [r_pvt[b]], inc=(j == len(grp) - 1))
                        v0 = grp[0][0]
                        cp(act if (g0 // 8) % 2 == 0 else dve, Vd[sl][:, v0:v0 + len(grp), :],
                           pvt[b][:, 0:len(grp), :], [r_pvt[b]], [r_Vd[sl]])

                items = []
                load_head(0)
                load_ga(0)
                for h in range(8):
                    sl = h % 2
                    gbank = {}
                    tiles = []
                    for d in (1, 4, 16):
                        nq = 16 // d
                        for r in range(d):
                            for i in range(nq + 1):
                                t0 = 1024 // d - 64 + 128 * i
                                jlo, jhi = max(i - 1, 0), min(i, nq - 1)
                                if i == 0:
                                    msk = mk[:, 1, 128:256]
                                elif i == nq:
                                    msk = mk[:, 1, 0:128]
                                else:
                                    msk = mk[:, 0, :]
                                pv = []
                                for j in range(jlo, jhi + 1):
                                    if d == 16:
                                        gkey, col = (d, r // 4, 0), 128 * (r % 4)
                                    else:
                                        gkey, col = (d, r, j // 4), 128 * (j % 4)
                                    if gkey not in gbank:
                                        gbank[gkey] = cnt["grp"] % 2
                                        cnt["grp"] += 1
                                    pv.append((j, gbank[gkey], col))
                                tiles.append(dict(d=d, r=r, i=i, nq=nq, ks0=r + d * t0, jlo=jlo,
                                                  ncol=128 * (jhi - jlo + 1), q0=r + d * 128 * jlo, msk=msk, pv=pv))
                    npairs = (len(tiles) + 1) // 2
                    for pi in range(npairs):
                        pair = tiles[2 * pi:2 * pi + 2]
                        sc = cnt["sc"] % 2
                        cnt["sc"] += 1
                        p = cnt["pt"] % NP
                        cnt["pt"] += 1
                        c0 = 0
                        for tl in pair:
                            tl["c0"] = c0
                            c0 += tl["ncol"]
                        wtot = c0
                        last_pair = (pi == npairs - 1)

                        def s0(h=h, sl=sl, pi=pi, sc=sc, pair=pair):
                            if pi == 0 and h == 0:
                                emit_vtrans(0)
                            if pi == 3 and h + 1 < 8:
                                load_head(h + 1)
                            if pi == 15 and h + 1 < 8:
                                emit_vtrans(h + 1)
                            if pi == 20 and h >= 1:
                                load_ga(h)
                            if pi == 0:
                                cp(pool, q4[:].rearrange("p (r m) -> p r m", r=4),
                                   qT[sl][:].rearrange("p (m r) -> p r m", r=4), [r_q[sl]], [r_q4])
                                cp(pool, q16[:].rearrange("p (r m) -> p r m", r=16),
                                   qT[sl][:].rearrange("p (m r) -> p r m", r=16), [r_q[sl]], [r_q16])
                            if pi == 22 and h == 7:
                                prefetch_TB0()
                            for tl in pair:
                                d, ks0, q0, ncol, c0 = tl["d"], tl["ks0"], tl["q0"], tl["ncol"], tl["c0"]
                                if d == 1:
                                    qmov, r_qm = qT[sl][:, q0:q0 + ncol], r_q[sl]
                                elif d == 4:
                                    qb = 512 * tl["r"] + 128 * tl["jlo"]
                                    qmov, r_qm = q4[:, qb:qb + ncol], r_q4
                                else:
                                    qb = 128 * tl["r"]
                                    qmov, r_qm = q16[:, qb:qb + ncol], r_q16
                                mm(psc[sc][:, c0:c0 + ncol], kT[sl][:, ks0:ks0 + 127 * d + 1:d],
                                   qmov, True, False, [r_qm, r_k[sl]], [r_psc[sc]], inc=False)
                                mm(psc[sc][:, c0:c0 + ncol], ident[:], tl["msk"], False, True, [r_mk, r_const], [r_psc[sc]])

                        def s1(sc=sc, p=p, wtot=wtot):
                            actv(PT[p][:, 0:wtot], psc[sc][:, 0:wtot], AF.Exp, [r_psc[sc]], [r_PT[p]], scale=ATT_SCALE)

                        def s2_(h=h, sl=sl, p=p, pair=pair):
                            accN, accD, r_accN, r_accD = accN_[sl], accD_[sl], r_accN_[sl], r_accD_[sl]
                            for tl in pair:
                                d, r, i, jlo, c0 = tl["d"], tl["r"], tl["i"], tl["jlo"], tl["c0"]
                                for (j, bank, col) in tl["pv"]:
                                    pc = c0 + 128 * (j - jlo)
                                    first = (i == j)
                                    last = (i == j + 1)
                                    mm(pnum[bank][:, col:col + 128], Vd[sl][:, vidx(d, r, i), :], PT[p][:, pc:pc + 128],
                                       first, last, [r_Vd[sl], r_PT[p]], [r_pnum[bank]], inc=last)
                                    mm(pden[bank][:, col:col + 128], ones[:], PT[p][:, pc:pc + 128],
                                       first, last, [r_const, r_PT[p]], [r_pden[bank]], inc=last)
                                    if last:
                                        if d == 1 and j % 4 == 3:
                                            g = j // 4
                                            cp(dve, accN[:, 512 * g:512 * g + 512], pnum[bank][:], [r_pnum[bank]], [r_accN])
                                            cp(act, accD[:, 512 * g:512 * g + 512], pden[bank][:], [r_pden[bank]], [r_accD])
                                        elif d == 4 and j == 3:
                                            vN = accN[:].rearrange("p (m r) -> p r m", r=4)[:, r, :]
                                            vD = accD[:].rearrange("p (m r) -> p r m", r=4)[:, r, :]
                                            tt(dve, vN, pnum[bank][:], vN, ALU.add, [r_pnum[bank], r_accN], [r_accN])
                                            tt(dve, vD, pden[bank][:], vD, ALU.add, [r_pden[bank], r_accD], [r_accD])
                                        elif d == 16 and r % 4 == 3:
                                            r0 = r - 3
                                            vN = accN[:].rearrange("p (m r) -> p r m", r=16)[:, r0:r0 + 4, :]
                                            vD = accD[:].rearrange("p (m r) -> p r m", r=16)[:, r0:r0 + 4, :]
                                            pn = pnum[bank][:].rearrange("p (r m) -> p r m", r=4)
                                            pd = pden[bank][:].rearrange("p (r m) -> p r m", r=4)
                                            tt(dve, vN, pn, vN, ALU.add, [r_pnum[bank], r_accN], [r_accN])
                                            tt(dve, vD, pd, vD, ALU.add, [r_pden[bank], r_accD], [r_accD])

                        def f1(h=h, sl=sl):
                            accD, r_accD = accD_[sl], r_accD_[sl]
                            r_d1 = fw.res()
                            fw.dma(sp, DEN1[h:h + 1, :], accD[0:1, :], [r_accD], [r_d1], key="den1")
                            fw.dma(sp, rsm[sl][:], DEN1[h].rearrange("(p j) -> p j", j=16), [r_d1], [r_rsm[sl]], key="den2")

                        def f2(h=h, sl=sl):
                            accD, r_accD = accD_[sl], r_accD_[sl]
                            r_d2 = fw.res()
                            recip(rsm[sl][:], rsm[sl][:], [r_rsm[sl]], [r_rsm[sl]])
                            fw.dma(sp, DEN2[h].rearrange("(p j) -> p j", j=16), rsm[sl][:], [r_rsm[sl]], [r_d2], key="den3")
                            fw.dma(sp, accD[:], DEN2[h:h + 1, :].broadcast_to([128, NT]), [r_d2], [r_accD], key="den4")

                        def f3(h=h, sl=sl):
                            accN, accD, r_accN, r_accD = accN_[sl], accD_[sl], r_accN_[sl], r_accD_[sl]
                            tt(dve, accN[:], accN[:], accD[:], ALU.mult, [r_accN, r_accD], [r_accN])
                            tt(pool, mixT[:, h, :], accN[:], gaT[0][:], ALU.mult, [r_accN, r_ga[0]], [r_mix])

                        if last_pair and h == 7:
                            late_fin.extend([f1, f2, f3])
                            items.append([s0, s1, s2_])
                        else:
                            items.append([s0, s1, s2_] + ([f1, f2, f3] if last_pair else []))
                run_pipeline(items, [0, 0, 2, 4, 10, 16])
                fw.barrier()

            with ExitStack() as sCD:
                wo01 = SB(sCD, "wo01", [128, 16, 512], BF16)
                r_wo = [fw.res() for _ in range(4)]

                def load_wo(g, wt, lbase):
                    for hh in range(2):
                        c0 = 512 * g + 256 * hh
                        l0 = lbase + 256 * hh
                        ldw(wt[:, :, l0:l0 + 256], w_out[:, c0:c0 + 256].rearrange("(c p) n -> p c n", p=128),
                            [r_wo[g]], key="wo%d" % g)

                with ExitStack() as s2:
                    sdb = SB(s2, "sdb", [128, 16, 1024], BF16)
                    r_sdb = [fw.res() for _ in range(8)]
                    TB = [TB0, SB(s2, "TB1", [128, 16, 2, 256], BF16)]
                    r_TB = [r_TB0, [fw.res() for _ in range(2)]]
                    GF = [SB(s2, "GF%d" % i, [128, 8, 512], BF16) for i in range(2)]
                    r_GF = [fw.res() for _ in range(2)]
                    XS = SB(s2, "XS", [128, 8, 2, 256], BF16)
                    r_XS = [fw.res() for _ in range(8)]
                    csm = SB(s2, "csm", [128, 2, 2, 256], BF16)
                    wfb = SB(s2, "wfb", [128, 4, 2, 256], BF16)
                    r_cw = fw.res()
                    px = [PS(s2, "px%d" % i, [128, 2, 256], F32) for i in range(4)]
                    r_px = [fw.res() for _ in range(4)]
                    pf = [PS(s2, "pf%d" % i, [128, 512], F32) for i in range(2)]
                    r_pf = [fw.res() for _ in range(2)]
                    po = [PS(s2, "po%d" % i, [128, 512], F32) for i in range(2)]
                    r_po = [fw.res() for _ in range(2)]
                    cnt = {"x": 0, "f": 0, "o": 0}
                    slices = [(par, kb) for par in range(2) for kb in range(4)]

                    def load_TB(i):
                        par, kb = slices[i]
                        sl = i % 2
                        for cs in range(2):
                            ld(TB[sl][:, :, cs, :], dft_d[par, kb][:, :, cs, :], [r_TB[sl][cs]], key="TB%d%d" % (sl, cs))

                    def load_GF(i):
                        par, kb = slices[i]
                        sl = i % 2
                        ld(GF[sl][:], GFS[:, :, 512 * kb:512 * kb + 512].rearrange("c p k -> p c k"),
                           [r_GF[sl]], key="GF%d" % sl)

                    def load_sdb(par, cc):
                        ld(sdb[:, :, 128 * cc:128 * cc + 128],
                           SDS[par, :, :, 128 * cc:128 * cc + 128].rearrange("t p c -> p t c"),
                           [r_sdb[cc]], key="sdb%d" % cc)

                    load_sdb(0, 0)
                    ld(csm[:], csm_d, [r_cw], key="cw")
                    ldw(wfb[:], w_f.rearrange("g (c p) e -> p g c e", p=128), [r_cw], key="cw2")
                    for cc in range(1, 8):
                        load_sdb(0, cc)
                    load_GF(0)
                    load_wo(0, wo01, 0)
                    Gm = SB(s2, "Gm", [128, 4, 2, 2, 256], BF16)
                    r_Gm = fw.res()

                    gm_pieces = [(g, cs, ck) for g in range(4) for cs in range(2) for ck in range(2)]

                    def build_Gm(lo, hi):
                        for (g, cs, ck) in gm_pieces[lo:hi]:
                            b = cnt["f"] % 2
                            cnt["f"] += 1
                            for ek in range(2):
                                mm(pf[b][:, 0:256], csm[:, cs, ek, 128 * ck:128 * ck + 128], wfb[:, g, ek, :],
                                   ek == 0, ek == 1, [r_cw], [r_pf[b]], inc=(ek == 1))
                            cp(act, Gm[:, g, cs, ck, :], pf[b][:, 0:256], [r_pf[b]], [r_Gm])

                    for i, (par, kb) in enumerate(slices):
                        sl = i % 2
                        if i < 3:
                            late_fin[i]()
                        if i + 1 < len(slices):
                            load_TB(i + 1)
                            load_GF(i + 1)
                        for cc in range(8):
                            b = cnt["x"] % 4
                            cnt["x"] += 1
                            for cs in range(2):
                                for t in range(16):
                                    mm(px[b][:, cs, :], sdb[:, t, 128 * cc:128 * cc + 128], TB[sl][:, t, cs, :],
                                       t == 0, t == 15, [r_sdb[cc], r_TB[sl][cs]], [r_px[b]], inc=(t == 15 and cs == 1))
                            if par == 0 and kb == 3:
                                load_sdb(1, cc)
                            if i == 0 and cc >= 2:
                                build_Gm(3 * (cc - 2), min(16, 3 * (cc - 2) + 3))
                            cp(act if cc % 2 == 0 else dve, XS[:, cc, :, :], px[b][:], [r_px[b]], [r_XS[cc]])
                        for g in range(4):
                            for oc in range(2):
                                b = cnt["o"] % 2
                                cnt["o"] += 1
                                n = 0
                                for ck in range(2):
                                    for cs in range(2):
                                        mm(po[b][:, 0:256], Gm[:, g, cs, ck, 128 * oc:128 * oc + 128], XS[:, 2 * g + ck, cs, :],
                                           n == 0, n == 3, [r_Gm, r_XS[2 * g + ck]], [r_po[b]], inc=(n == 3))
                                        n += 1
                                ch = 2 * g + oc
                                k0 = 512 * kb + par
                                tt(dve, mixT[:, 8 + ch, k0:512 * kb + 512:2], po[b][:, 0:256], GF[sl][:, ch, par:512:2],
                                   ALU.mult, [r_po[b], r_GF[sl]], [r_mix])
                    fw.barrier()

                if debug:
                    for c in range(16):
                        stq(MIXS[c], mixT[:, c, :], [r_mix], key="dbg")
                    fw.barrier()

                with ExitStack() as s2:
                    wo123 = SB(s2, "wo123", [128, 16, 1536], BF16)
                    for g in range(1, 4):
                        load_wo(g, wo123, 512 * (g - 1))
                    NX = 4
                    xr = [SB(s2, "xr%d" % i, [128, 512], F32) for i in range(NX)]
                    r_xr = [fw.res() for _ in range(NX)]
                    yo = [SB(s2, "yo%d" % i, [128, 512], F32) for i in range(NX)]
                    r_yo = [fw.res() for _ in range(NX)]
                    py = [PS(s2, "py%d" % i, [128, 512], F32) for i in range(4)]
                    r_py = [fw.res() for _ in range(4)]
                    items = []
                    n = 0
                    for g in range(4):
                        wt = wo01 if g == 0 else wo123
                        l0 = 0 if g == 0 else 512 * (g - 1)
                        for t in range(16):
                            xi = n % NX
                            b = n % 4
                            n += 1
                            rows = slice(128 * t, 128 * t + 128)
                            cols = slice(512 * g, 512 * g + 512)

                            def s0(t=t, g=g, xi=xi, b=b, wt=wt, l0=l0, rows=rows, cols=cols):
                                ld(xr[xi][:], x_own[rows, cols], [r_xr[xi]], key="xr%d" % xi)
                                for c in range(16):
                                    mm(py[b][:], mixT[:, c, 128 * t:128 * t + 128], wt[:, c, l0:l0 + 512],
                                       c == 0, c == 15, [r_wo[g], r_mix], [r_py[b]], inc=(c == 15))

                            def s1(xi=xi, b=b, rows=rows, cols=cols):
                                tt(dve, yo[xi][:], py[b][:], xr[xi][:], ALU.add, [r_py[b], r_xr[xi]], [r_yo[xi]])
                                stq(y[rows, cols], yo[xi][:], [r_yo[xi]], key="yo%d" % xi)

                            items.append([s0, s1])
                    run_pipeline(items, [0, 1])
                    fw.barrier()
        fw.flush()
    return nc


def _rope_tables(pos):
    half = 16
    inv_freq = 1.0 / (500000.0 ** (np.arange(half, dtype=np.float64) / half))
    ang = pos.astype(np.float64)[None, :] * inv_freq[:, None]
    cos = np.cos(ang)
    sin = np.sin(ang)
    cosT = np.concatenate([cos, cos], axis=0)
    sinT = np.concatenate([-sin, sin], axis=0)
    return np.stack([cosT, sinT]).astype(np.float32)


def _dft_tables(kind):
    n = np.arange(2048, dtype=np.int64)
    j = np.arange(1024, dtype=np.int64)
    out = np.empty((2, 2, 2048, 1024), dtype=np.float32)
    for par in range(2):
        if kind == "A":
            N, k, sgn = 4096, 2 * j + par, 1.0
        elif kind == "B":
            N, k, sgn = 4096, 2048 + 2 * j + par, (1.0 if par == 0 else -1.0)
        else:
            N, k, sgn = 2048, 2 * j + par, 1.0
        ph = (n[:, None] * k[None, :]) % N
        th = 2.0 * np.pi * ph.astype(np.float64) / N
        out[par, 0] = sgn * np.cos(th)
        out[par, 1] = -sgn * np.sin(th)
    o = out.reshape(2, 2, 16, 128, 4, 256).transpose(0, 4, 3, 2, 1, 5)
    return np.ascontiguousarray(o).astype(BF)


def _csm_table(S):
    scale = 1.0 / math.sqrt(S * 256.0)
    c = np.arange(256, dtype=np.int64)
    ph = (c[:, None] * c[None, :]) % 256
    th = 2.0 * np.pi * ph.astype(np.float64) / 256
    m = np.stack([np.cos(th), np.sin(th)]) * scale
    o = m.reshape(2, 2, 128, 256).transpose(2, 0, 1, 3)
    return np.ascontiguousarray(o).astype(BF)


def _masks(kind):
    a = np.arange(128)[:, None]
    b = np.arange(256)[None, :]
    band = ((b - a) >= 0) & ((b - a) <= 128)
    left_valid = kind == "B"
    right_valid = kind == "A"
    edge = band.copy()
    if not left_valid:
        edge[:64, 128:256] = False
    if not right_valid:
        edge[64:, 0:128] = False
    m = np.stack([band, edge], axis=1).astype(np.float32)
    return ((m - 1.0) * 30000.0).astype(BF)


_NC_CACHE = {}
_CONST_CACHE = {}


def _consts(kind):
    if kind not in _CONST_CACHE:
        S = 4096 if kind in ("A", "B") else 2048
        _CONST_CACHE[kind] = dict(dft=_dft_tables(kind), csm=_csm_table(S), masks=_masks(kind))
    return _CONST_CACHE[kind]


def make_in_maps(x_prompt, x_sample, rms_gain, w_in, q_norm_gain, k_norm_gain, w_fourier, w_out):
    ident = np.eye(128, dtype=np.float32).astype(BF)
    ones = np.ones((128, 128), dtype=np.float32).astype(BF)
    psw = np.zeros((128, 128), dtype=np.float32)
    for m in range(16):
        psw[m + 16, m] = 1.0
        psw[m, m + 16] = 1.0
    psw = psw.astype(BF)
    shared = dict(
        rms_g=np.ascontiguousarray(rms_gain[0].reshape(1, DM)),
        w_in=np.ascontiguousarray(w_in[0]),
        qg=np.ascontiguousarray(q_norm_gain[0].reshape(128, 1)),
        kg=np.ascontiguousarray(k_norm_gain[0].reshape(128, 1)),
        w_f=np.ascontiguousarray(w_fourier[0]),
        w_out=np.ascontiguousarray(w_out[0]),
        ident=ident, ones=ones, pswap=psw,
    )
    zeros = np.zeros((NT, DM), dtype=np.float32)
    in_maps = []
    for c in range(8):
        if c < 4:
            b, half = c // 2, c % 2
            kind = "A" if half == 0 else "B"
            xo = x_prompt[b, 2048 * half:2048 * half + 2048]
            xt = x_prompt[b, 2048 * (1 - half):2048 * (1 - half) + 2048]
            h0 = 2048 if half == 0 else 1024
            xe = xt
            hoff = 1024 + (h0 - 2048 * (1 - half))
            pos_own = 2048 * half + np.arange(NT)
            pos_oth = np.concatenate([h0 + np.arange(1024), np.zeros(1024, dtype=np.int64)])
        else:
            kind = "S"
            xo = x_sample[c - 4]
            xe = zeros
            hoff = 1024
            pos_own = np.arange(NT)
            pos_oth = np.zeros(NT, dtype=np.int64)
        cst = _consts(kind)
        m = dict(shared)
        m.update(
            x_own=np.ascontiguousarray(xo), x_ext=np.ascontiguousarray(xe),
            hoff=np.array([[hoff]], dtype=np.int32),
            rope_own=_rope_tables(pos_own), rope_oth=_rope_tables(pos_oth),
            masks=cst["masks"], dft=cst["dft"], csm=cst["csm"],
        )
        in_maps.append(m)
    return in_maps


def kernel(x_prompt, x_sample, rms_gain, w_in, q_norm_gain, k_norm_gain, w_fourier, w_out):
    args = [np.asarray(a) for a in (x_prompt, x_sample, rms_gain, w_in, q_norm_gain, k_norm_gain, w_fourier, w_out)]
    in_maps = make_in_maps(*args)
    if "nc" not in _NC_CACHE:
        _NC_CACHE["nc"] = build()
    res = run_bass_kernel_spmd(_NC_CACHE["nc"], in_maps, core_ids=list(range(8)))
    outs = [np.asarray(r["y"], dtype=np.float32) for r in res.results]
    y_prompt = np.stack([np.concatenate([outs[0], outs[1]], axis=0),
                         np.concatenate([outs[2], outs[3]], axis=0)], axis=0)
    y_sample = np.stack(outs[4:8], axis=0)
    return (y_prompt, y_sample)
```

```python
import math
from contextlib import ExitStack

import numpy as np
import ml_dtypes

import concourse.bass as bass
import concourse.mybir as mybir
from concourse.bass_utils import run_bass_kernel_spmd

F32 = mybir.dt.float32
BF16 = mybir.dt.bfloat16
AF = mybir.ActivationFunctionType
ALU = mybir.AluOpType

NT = 2048
DM = 2048
EPS = 1e-6
ATT_SCALE = 128 ** -0.5
BF = ml_dtypes.bfloat16


class Ev:
    __slots__ = ("sem", "val", "sid")

    def __init__(self, sem, val, sid):
        self.sem = sem
        self.val = val
        self.sid = sid


class Res:
    __slots__ = ("name", "w", "r")

    def __init__(self, name):
        self.name = name
        self.w = None
        self.r = {}


class Eng:
    def __init__(self, fw, name, is_pe=False):
        self.name = name
        self.is_pe = is_pe
        self.sem = fw.new_sem("e_" + name)
        self.sid = fw.sid(self.sem)
        self.count = 0
        self.seen = {}
        self.prog = []
        self.nins = 0


class FW:
    def __init__(self, nc, stack):
        self.nc = nc
        self.stack = stack
        self._sids = {}
        self.nsem = 0
        self.pe = Eng(self, "tensor", is_pe=True)
        self.act = Eng(self, "scalar")
        self.dve = Eng(self, "vector")
        self.pool = Eng(self, "gpsimd")
        self.sp = Eng(self, "sync")
        self.engs = [self.pe, self.act, self.dve, self.pool, self.sp]
        self.dsems = {}
        self.dcount = {}

    def new_sem(self, name):
        self.nsem += 1
        return self.stack.enter_context(self.nc.semaphore(name))

    def sid(self, sem):
        k = id(sem)
        if k not in self._sids:
            self._sids[k] = len(self._sids)
        return self._sids[k]

    def res(self, name="r"):
        return Res(name)

    def _wait(self, eng, ev):
        if ev is None:
            return
        if eng.is_pe and ev.sid == eng.sid:
            return
        if eng.seen.get(ev.sid, 0) >= ev.val:
            return
        eng.seen[ev.sid] = ev.val
        sem, val = ev.sem, ev.val
        eng.prog.append(lambda h: h.wait_ge(sem, val))

    def _deps(self, eng, reads, writes):
        for r in reads:
            self._wait(eng, r.w)
        for w in writes:
            self._wait(eng, w.w)
            for e in w.r.values():
                self._wait(eng, e)

    def _commit(self, ev, reads, writes):
        for r in reads:
            old = r.r.get(ev.sid)
            if old is None or old.val < ev.val:
                r.r[ev.sid] = ev
        for w in writes:
            w.w = ev
            w.r = {}

    def op(self, eng, fn, reads=(), writes=(), inc=True):
        self._deps(eng, reads, writes)
        eng.nins += 1
        if inc:
            eng.count += 1
            ev = Ev(eng.sem, eng.count, eng.sid)
            sem = eng.sem
            eng.prog.append(lambda h: fn(h).then_inc(sem, 1))
        else:
            ev = Ev(eng.sem, eng.count + 1, eng.sid)
            eng.prog.append(lambda h: fn(h))
        self._commit(ev, reads, writes)
        return ev

    def dma(self, q, out, in_, reads=(), writes=(), key=None, **kw):
        self._deps(q, reads, writes)
        if key not in self.dsems:
            self.dsems[key] = self.new_sem("d_" + key)
            self.dcount[key] = 0
        sem = self.dsems[key]
        self.dcount[key] += 16
        ev = Ev(sem, self.dcount[key], self.sid(sem))
        q.nins += 1
        q.prog.append(lambda h: h.dma_start(out=out, in_=in_, **kw).then_inc(sem, 16))
        self._commit(ev, reads, writes)
        return ev

    def dma_fn(self, q, fn, reads=(), writes=(), key=None):
        self._deps(q, reads, writes)
        if key not in self.dsems:
            self.dsems[key] = self.new_sem("d_" + key)
            self.dcount[key] = 0
        sem = self.dsems[key]
        self.dcount[key] += 16
        ev = Ev(sem, self.dcount[key], self.sid(sem))
        q.nins += 1
        q.prog.append(lambda h: fn(h).then_inc(sem, 16))
        self._commit(ev, reads, writes)
        return ev

    def barrier(self):
        evs = []
        for key, sem in self.dsems.items():
            evs.append(Ev(sem, self.dcount[key], self.sid(sem)))
        for e in self.engs:
            if e.count > 0:
                evs.append(Ev(e.sem, e.count, e.sid))
        for e in self.engs:
            for ev in evs:
                if ev.sid != e.sid:
                    self._wait(e, ev)

    def flush(self):
        nc = self.nc
        for key, sem in self.dsems.items():
            self._wait(self.sp, Ev(sem, self.dcount[key], self.sid(sem)))
        for e in self.engs:
            if e is not self.sp and e.count > 0:
                self._wait(self.sp, Ev(e.sem, e.count, e.sid))
        progs = {e.name: e.prog for e in self.engs}
        for e in self.engs:
            e.prog = []

        def run(name):
            def body(h):
                for c in progs[name]:
                    c(h)
            return body

        with nc.Block(no_gpsimd_drain=True) as block:
            block.tensor(run("tensor"))
            block.scalar(run("scalar"))
            block.vector(run("vector"))
            block.gpsimd(run("gpsimd"))
            block.sync(run("sync"))


def build(debug=False):
    nc = bass.Bass("TRN2", target_bir_lowering=False)

    def D(name, shape, dt, kind="ExternalInput"):
        return nc.dram_tensor(name, shape, dt, kind=kind).ap()

    x_own = D("x_own", [NT, DM], F32)
    x_ext = D("x_ext", [NT, DM], F32)
    hoff_d = D("hoff", [1, 1], mybir.dt.int32)
    rms_g = D("rms_g", [1, DM], F32)
    w_in = D("w_in", [DM, 6144], F32)
    qg = D("qg", [128, 1], F32)
    kg = D("kg", [128, 1], F32)
    w_f = D("w_f", [4, 256, 256], F32)
    w_out = D("w_out", [DM, DM], F32)
    ident_d = D("ident", [128, 128], BF16)
    ones_d = D("ones", [128, 128], BF16)
    pswap_d = D("pswap", [128, 128], BF16)
    rope_own = D("rope_own", [2, 32, NT], F32)
    rope_oth = D("rope_oth", [2, 32, NT], F32)
    masks_d = D("masks", [128, 2, 256], BF16)
    dft_d = D("dft", [2, 4, 128, 16, 2, 256], BF16)
    csm_d = D("csm", [128, 2, 2, 256], BF16)
    y = D("y", [NT, DM], F32, kind="ExternalOutput")

    sk = "ExternalOutput" if debug else "Internal"
    KS = D("KS", [8, 128, 4096], BF16, kind=sk)
    VS = D("VS", [8, 128, 4096], BF16, kind=sk)
    QS = D("QS", [8, 128, NT], BF16, kind=sk)
    GAS = D("GAS", [8, 128, NT], BF16, kind=sk)
    GFS = D("GFS", [8, 128, NT], BF16, kind=sk)
    UOS = D("UOS", [16, 128, 1024], BF16, kind=sk)
    SDS = D("SDS", [2, 16, 128, 1024], BF16, kind=sk)
    DEN1 = D("DEN1", [8, 2048], F32, kind=sk)
    DEN2 = D("DEN2", [8, 2048], F32, kind=sk)
    MIXS = D("MIXS", [16, 128, NT], BF16, kind="ExternalOutput") if debug else None

    with ExitStack() as st:
        fw = FW(nc, st)
        pe, act, dve, pool, sp = fw.pe, fw.act, fw.dve, fw.pool, fw.sp

        def mm(out, lhsT, rhs, start, stop, R=(), W=(), inc=True):
            return fw.op(pe, lambda h: h.matmul(out, lhsT=lhsT, rhs=rhs, start=start, stop=stop), R, W, inc)

        def tr(out, in_, idn, R=(), W=(), inc=True):
            return fw.op(pe, lambda h: h.transpose(out=out, in_=in_, identity=idn), R, W, inc)

        def actv(out, in_, func, R=(), W=(), scale=None, bias=None, accum=None):
            kw = {}
            if scale is not None:
                kw["scale"] = scale
            if bias is not None:
                kw["bias"] = bias
            if accum is not None:
                kw["accum_out"] = accum
            return fw.op(act, lambda h: h.activation(out=out, in_=in_, func=func, **kw), R, W)

        def tt(eng, out, a, b, op, R=(), W=()):
            return fw.op(eng, lambda h: h.tensor_tensor(out=out, in0=a, in1=b, op=op), R, W)

        def stt(out, in0, scalar, in1, op0, op1, R=(), W=()):
            return fw.op(dve, lambda h: h.scalar_tensor_tensor(out=out, in0=in0, scalar=scalar, in1=in1,
                                                               op0=op0, op1=op1), R, W)

        def cp(eng, out, in_, R=(), W=()):
            if eng is act:
                return fw.op(act, lambda h: h.copy(out=out, in_=in_), R, W)
            return fw.op(eng, lambda h: h.tensor_copy(out=out, in_=in_), R, W)

        def recip(out, in_, R=(), W=()):
            return fw.op(dve, lambda h: h.reciprocal(out=out, in_=in_), R, W)

        def ld(out, in_, W=(), key=None, **kw):
            return fw.dma(sp, out, in_, (), W, key=key, **kw)

        def stq(out, in_, R=(), key=None, **kw):
            return fw.dma(sp, out, in_, R, (), key=key, **kw)

        uniq = [0]

        def SB(stack, name, shape, dt):
            uniq[0] += 1
            return stack.enter_context(nc.sbuf_tensor("sb%d_%s" % (uniq[0], name), shape, dt))

        def PS(stack, name, shape, dt):
            uniq[0] += 1
            return stack.enter_context(nc.psum_tensor("ps%d_%s" % (uniq[0], name), shape, dt))

        ident = SB(st, "ident", [128, 128], BF16)
        ones = SB(st, "ones", [128, 128], BF16)
        pswap = SB(st, "pswap", [128, 128], BF16)
        qg_t = SB(st, "qg_t", [128, 1], F32)
        kg_t = SB(st, "kg_t", [128, 1], F32)
        r_const = fw.res("const")
        ld(ident[:], ident_d, [r_const], key="c0")
        ld(ones[:], ones_d, [r_const], key="c0")
        ld(pswap[:], pswap_d, [r_const], key="c0")
        ld(qg_t[:], qg, [r_const], key="c0")
        ld(kg_t[:], kg, [r_const], key="c0")

        def run_pipeline(items, skew):
            n = len(items)
            ns = len(skew)
            for step in range(n + max(skew)):
                for s in range(ns):
                    b = step - skew[s]
                    if 0 <= b < n and s < len(items[b]) and items[b][s] is not None:
                        items[b][s]()

        def ldw(out, in_, W=(), key=None):
            return fw.dma(pool, out, in_, (), W, key=key)

        def phase_norm(x_src, hT, ntiles, col0=0, xt=None, r_xt=None, preloaded=0):
            with ExitStack() as s2:
                NXT = 3
                if xt is None:
                    xt = [SB(s2, "xt%d" % i, [128, DM], F32) for i in range(NXT)]
                    r_xt = [fw.res() for _ in range(NXT)]
                junk = SB(s2, "junk", [128, DM], BF16)
                r_junk = fw.res()
                xn = [SB(s2, "xn%d" % i, [128, DM], BF16) for i in range(NXT)]
                r_xn = [fw.res() for _ in range(NXT)]
                gb = SB(s2, "gb", [128, DM], F32)
                r_gb = fw.res()
                ss = [SB(s2, "ss%d" % i, [128, 1], F32) for i in range(NXT)]
                r_ss = [fw.res() for _ in range(NXT)]
                ptb = [PS(s2, "ptb%d" % i, [128, 8, 128], BF16) for i in range(6)]
                r_ptb = [fw.res() for _ in range(6)]
                ld(gb[:], rms_g.broadcast_to([128, DM]), [r_gb], key="gb")
                items = []
                for t in range(ntiles):
                    sl = t % NXT

                    def s0(t=t, sl=sl):
                        if t >= preloaded:
                            ld(xt[sl][:], x_src[128 * t:128 * t + 128, :], [r_xt[sl]], key="xt%d" % sl)

                    def s1(t=t, sl=sl):
                        actv(junk[:], xt[sl][:], AF.Square, [r_xt[sl]], [r_junk, r_ss[sl]], accum=ss[sl][:])
                        actv(ss[sl][:], ss[sl][:], AF.Sqrt, [r_ss[sl]], [r_ss[sl]], scale=1.0 / DM, bias=EPS)
                        recip(ss[sl][:], ss[sl][:], [r_ss[sl]], [r_ss[sl]])
                        stt(xn[sl][:], xt[sl][:], ss[sl][:], gb[:], ALU.mult, ALU.mult,
                            [r_xt[sl], r_ss[sl], r_gb], [r_xn[sl]])

                    def s2_(t=t, sl=sl):
                        for half in range(2):
                            bi = (2 * t + half) % 6
                            for c in range(8):
                                cc = 8 * half + c
                                tr(ptb[bi][:, c, :], xn[sl][:, cc * 128:(cc + 1) * 128], ident[:],
                                   [r_xn[sl], r_const], [r_ptb[bi]], inc=(c == 7))

                    def s3(t=t, sl=sl):
                        for half in range(2):
                            bi = (2 * t + half) % 6
                            cp(act if half == 0 else dve, hT[:, 8 * half:8 * half + 8, col0 + 128 * t:col0 + 128 * t + 128],
                               ptb[bi][:], [r_ptb[bi]], [])

                    items.append([s0, s1, s2_, s3])
                run_pipeline(items, [0, 1, 2, 3])
                fw.barrier()

        NW = 4

        NPRE = 1

        def prefetch_units(units, wbf, r_wbf):
            for i in range(min(NPRE, len(units))):
                kind, idx, col0 = units[i]
                ldw(wbf[i % NW][:], w_in[:, col0:col0 + 256].rearrange("(c p) n -> p c n", p=128),
                    [r_wbf[i % NW]], key="wbf%d" % (i % NW))

        def phase_proj(hT, units, rope_d, own, wbf, r_wbf, hook=None):
            with ExitStack() as s2:
                cosT = SB(s2, "cosT", [32, NT], F32)
                sinT = SB(s2, "sinT", [32, NT], F32)
                r_rope = fw.res()
                ld(cosT[:], rope_d[0], [r_rope], key="rope")
                ld(sinT[:], rope_d[1], [r_rope], key="rope")
                NS = 3
                NO = 5
                sq = [SB(s2, "sq%d" % i, [128, 512], BF16) for i in range(NS)]
                r_sq = [fw.res() for _ in range(NS)]
                rs = [SB(s2, "rs%d" % i, [128, 512], F32) for i in range(NS)]
                r_rs = [fw.res() for _ in range(NS)]
                ot = [SB(s2, "ot%d" % i, [128, 512], BF16) for i in range(NO)]
                r_ot = [fw.res() for _ in range(NO)]
                t1 = [SB(s2, "t1_%d" % i, [32, 512], F32) for i in range(NS)]
                r_t1 = [fw.res() for _ in range(NS)]
                t2 = [SB(s2, "t2_%d" % i, [32, 512], F32) for i in range(NS)]
                r_t2 = [fw.res() for _ in range(NS)]
                nub = 2 if own else 1
                ub = [SB(s2, "ub%d" % i, [128, 16, 256], BF16) for i in range(nub)]
                r_ub = [[fw.res() for _ in range(2)] for _ in range(nub)]
                uo2 = [SB(s2, "uo%d" % i, [128, 16, 256], BF16) for i in range(2)] if own else None
                r_uo2 = [fw.res() for _ in range(2)]
                tok0 = 0 if own else 1024

                def load_uo(j):
                    kind_, idx_, _c = units[j]
                    ld(uo2[idx_ % 2][:], UOS[:, :, 256 * idx_:256 * idx_ + 256].rearrange("t p c -> p t c"),
                       [r_uo2[idx_ % 2]], key="uo%d" % (idx_ % 2))
                NA = 4
                pacc = [PS(s2, "pacc%d" % i, [128, 512], F32) for i in range(NA)]
                r_pacc = [fw.res() for _ in range(NA)]
                pss = [PS(s2, "pss%d" % i, [128, 512], F32) for i in range(2)]
                r_pss = [fw.res() for _ in range(2)]
                ppq = [PS(s2, "ppq%d" % i, [128, 512], F32) for i in range(2)]
                r_ppq = [fw.res() for _ in range(2)]
                cnt = {"acc": 0, "pp": 0, "ot": 0}

                def load_unit(i):
                    kind, idx, col0 = units[i]
                    sl = i % NW
                    ldw(wbf[sl][:], w_in[:, col0:col0 + 256].rearrange("(c p) n -> p c n", p=128),
                        [r_wbf[sl]], key="wbf%d" % sl)

                for i in range(NPRE, min(NW - 1, len(units))):
                    load_unit(i)
                r_halo = fw.res()
                if not own:
                    state = {}

                    def halo_dma(h, c4):
                        if "c" not in state:
                            reg = h.alloc_register("hoff")
                            h.reg_load(reg, hoff_d[0:1, 0:1])
                            state["c"] = h.snap(reg, min_val=1024, max_val=2048)
                        return h.dma_start(out=hT[:, 4 * c4:4 * c4 + 4, 0:1024],
                                           in_=hT[:, 4 * c4:4 * c4 + 4, bass.ds(state["c"], 1024)])

                    def issue_halo():
                        for c4 in range(4):
                            fw.dma_fn(sp, lambda h, c4=c4: halo_dma(h, c4), (), [r_halo], key="halo")
                else:
                    issue_halo = None
                items = []
                for i, (kind, idx, col0) in enumerate(units):
                    sl = i % NW
                    first_block = True
                    if kind == "u":
                        cs = slice(256 * idx, 256 * idx + 256)
                        for t in range(16):
                            a = cnt["acc"] % NA
                            cnt["acc"] += 1

                            def s0(i=i, sl=sl, t=t, a=a, fb=first_block, idx=idx):
                                if fb:
                                    if i == 2 and issue_halo is not None:
                                        issue_halo()
                                    if i + NW - 1 < len(units):
                                        load_unit(i + NW - 1)
                                    if own and i + 1 < len(units) and units[i + 1][0] == "u":
                                        load_uo(i + 1)
                                tk = tok0 + 128 * t
                                for c in range(16):
                                    mm(pacc[a][:, 0:256], hT[:, c, tk:tk + 128], wbf[sl][:, c, :],
                                       c == 0, c == 15, [r_wbf[sl]], [r_pacc[a]], inc=(c == 15))

                            def s1(t=t, a=a, cs=cs, idx=idx):
                                uo, r_uo = (uo2[idx % 2], r_uo2[idx % 2]) if own else (None, None)
                                if own:
                                    tt(dve, ub[0][:, t, :], pacc[a][:, 0:256], uo[:, t, :], ALU.add,
                                       [r_pacc[a], r_uo], [r_ub[0][t // 8]])
                                    tt(dve, ub[1][:, t, :], pacc[a][:, 0:256], uo[:, t, :], ALU.subtract,
                                       [r_pacc[a], r_uo], [r_ub[1][t // 8]])
                                else:
                                    cp(act, ub[0][:, t, :], pacc[a][:, 0:256], [r_pacc[a]], [r_ub[0][t // 8]])
                                if t % 8 == 7:
                                    hs = slice(t - 7, t + 1)
                                    hh = t // 8
                                    if own:
                                        stq(SDS[0, hs, :, cs].rearrange("t p c -> p t c"), ub[0][:, hs, :], [r_ub[0][hh]], key="ub0%d" % hh)
                                        stq(SDS[1, hs, :, cs].rearrange("t p c -> p t c"), ub[1][:, hs, :], [r_ub[1][hh]], key="ub1%d" % hh)
                                    else:
                                        stq(UOS[hs, :, cs].rearrange("t p c -> p t c"), ub[0][:, hs, :], [r_ub[0][hh]], key="ub0%d" % hh)

                            items.append([s0, s1])
                            first_block = False
                        continue
                    for sub in range(2):
                        head = 2 * idx + sub
                        for b in range(4 if own else 2):
                            a = cnt["acc"] % NA
                            cnt["acc"] += 1
                            o = cnt["ot"] % NO
                            cnt["ot"] += 1
                            p = cnt["pp"] % NS
                            p2 = cnt["pp"] % 2
                            if kind in ("q", "k"):
                                cnt["pp"] += 1
                            tsl = slice(512 * b, 512 * b + 512)
                            if kind == "q":
                                dst = QS[head, :, tsl]
                            elif kind == "ga":
                                dst = GAS[head, :, tsl]
                            elif kind == "gf":
                                dst = GFS[head, :, tsl]
                            else:
                                T = KS if kind == "k" else VS
                                if own:
                                    dst = T[head, :, 1024 + 512 * b:1024 + 512 * b + 512]
                                else:
                                    dst = (T[head, :, 512 * b:512 * b + 512],
                                           T[head, :, 3072 + 512 * b:3072 + 512 * b + 512])
                            gain = qg_t if kind == "q" else kg_t

                            def s0(i=i, sl=sl, sub=sub, b=b, a=a, fb=first_block):
                                if fb and hook is not None and hook[0] == i:
                                    hook[1]()
                                if fb and i + NW - 1 < len(units):
                                    load_unit(i + NW - 1)
                                if fb and own and i + 1 < len(units) and units[i + 1][0] == "u":
                                    load_uo(i + 1)
                                for c in range(16):
                                    mm(pacc[a][:], wbf[sl][:, c, 128 * sub:128 * sub + 128],
                                       hT[:, c, 512 * b:512 * b + 512],
                                       c == 0, c == 15, [r_wbf[sl], r_halo], [r_pacc[a]], inc=(c == 15))

                            if kind in ("q", "k"):
                                def s1(a=a, p=p):
                                    actv(sq[p][:], pacc[a][:], AF.Square, [r_pacc[a]], [r_sq[p]])

                                def s2_(p=p, p2=p2):
                                    mm(pss[p2][:], ones[:], sq[p][:], True, True, [r_sq[p], r_const], [r_pss[p2]])

                                def s3(a=a, p=p, p2=p2, o=o, gain=gain):
                                    actv(rs[p][:], pss[p2][:], AF.Ln, [r_pss[p2]], [r_rs[p]], scale=1.0 / 128, bias=EPS)
                                    actv(rs[p][:], rs[p][:], AF.Exp, [r_rs[p]], [r_rs[p]], scale=-0.5)
                                    stt(ot[o][:], pacc[a][:], gain[:], rs[p][:], ALU.mult, ALU.mult,
                                        [r_pacc[a], r_rs[p], r_const], [r_ot[o]])

                                def s4(p2=p2, o=o):
                                    mm(ppq[p2][:], pswap[:], ot[o][:], True, True, [r_ot[o], r_const], [r_ppq[p2]])

                                def s5(p=p, p2=p2, o=o, tsl=tsl, dst=dst):
                                    tt(pool, t1[p][:], ot[o][0:32, :], cosT[:, tsl], ALU.mult,
                                       [r_ot[o], r_rope], [r_t1[p]])
                                    tt(dve, t2[p][:], ppq[p2][0:32, :], sinT[:, tsl], ALU.mult,
                                       [r_ppq[p2], r_rope], [r_t2[p]])
                                    tt(pool, ot[o][0:32, :], t1[p][:], t2[p][:], ALU.add,
                                       [r_t1[p], r_t2[p]], [r_ot[o]])
                                    for dd in (dst if isinstance(dst, tuple) else (dst,)):
                                        stq(dd, ot[o][:], [r_ot[o]], key="ot%d" % o)

                                items.append([s0, s1, s2_, s3, s4, s5])
                            else:
                                def s1(a=a, o=o, kind=kind, dst=dst):
                                    if kind == "v":
                                        cp(act, ot[o][:], pacc[a][:], [r_pacc[a]], [r_ot[o]])
                                    else:
                                        actv(ot[o][:], pacc[a][:], AF.Silu, [r_pacc[a]], [r_ot[o]])
                                    for dd in (dst if isinstance(dst, tuple) else (dst,)):
                                        stq(dd, ot[o][:], [r_ot[o]], key="ot%d" % o)

                                items.append([s0, s1])
                            first_block = False
                run_pipeline(items, [0, 0, 1, 1, 2, 2])
                fw.barrier()

        with ExitStack() as sA:
            hT = SB(sA, "hT", [128, 16, NT + 1024], BF16)
            wbfA = [SB(sA, "wbf%d" % i, [128, 16, 256], BF16) for i in range(NW)]
            r_wbfA = [fw.res() for _ in range(NW)]
            units_o = [("u", i, 4096 + 256 * i) for i in range(4)] + \
                      [("k", i, 1024 + 256 * i) for i in range(4)] + \
                      [("v", i, 2048 + 256 * i) for i in range(4)]
            prefetch_units(units_o, wbfA, r_wbfA)
            phase_norm(x_ext, hT, 16, col0=1024)
            sXT = ExitStack()
            sXT.__enter__()
            xtO = [SB(sXT, "xtO%d" % i, [128, DM], F32) for i in range(3)]
            r_xtO = [fw.res() for _ in range(3)]

            def pre_x():
                for t in range(3):
                    ld(xtO[t][:], x_own[128 * t:128 * t + 128, :], [r_xtO[t]], key="xt%d" % t)

            phase_proj(hT, units_o, rope_oth, False, wbfA, r_wbfA, hook=(10, pre_x))
            units_w = [("q", i, 0 + 256 * i) for i in range(4)] + \
                      [("k", i, 1024 + 256 * i) for i in range(4)] + \
                      [("v", i, 2048 + 256 * i) for i in range(4)] + \
                      [("ga", i, 3072 + 256 * i) for i in range(4)] + \
                      [("u", i, 4096 + 256 * i) for i in range(4)] + \
                      [("gf", i, 5120 + 256 * i) for i in range(4)]
            prefetch_units(units_w, wbfA, r_wbfA)
            phase_norm(x_own, hT, 16, xt=xtO, r_xt=r_xtO, preloaded=3)
            sXT.__exit__(None, None, None)
            phase_proj(hT, units_w, rope_own, True, wbfA, r_wbfA)

        with ExitStack() as sM:
            mixT = SB(sM, "mixT", [128, 16, NT], BF16)
            r_mix = fw.res("mixT")
            TB0 = SB(sM, "TB0", [128, 16, 2, 256], BF16)
            r_TB0 = [fw.res() for _ in range(2)]

            def prefetch_TB0():
                for cs in range(2):
                    ld(TB0[:, :, cs, :], dft_d[0, 0][:, :, cs, :], [r_TB0[cs]], key="TB0%d" % cs)

            accN7 = SB(sM, "accN7", [128, NT], F32)
            accD7 = SB(sM, "accD7", [128, NT], F32)
            gaT0 = SB(sM, "gaT0", [128, NT], BF16)
            rsmO = [SB(sM, "rsmO%d" % i, [128, 16], F32) for i in range(2)]
            late_fin = []

            with ExitStack() as s2:
                qT = [SB(s2, "qT%d" % i, [128, NT], BF16) for i in range(2)]
                kT = [SB(s2, "kT%d" % i, [128, 4096], BF16) for i in range(2)]
                vT = [SB(s2, "vT%d" % i, [128, 4096], BF16) for i in range(2)]
                gaT = [gaT0]
                r_q = [fw.res() for _ in range(2)]
                r_k = [fw.res() for _ in range(2)]
                r_v = [fw.res() for _ in range(2)]
                r_ga = [fw.res() for _ in range(1)]
                rsm = rsmO
                r_rsm = [fw.res() for _ in range(2)]
                q4 = SB(s2, "q4", [128, NT], BF16)
                q16 = SB(s2, "q16", [128, NT], BF16)
                r_q4 = fw.res()
                r_q16 = fw.res()
                Vd = [SB(s2, "Vd%d" % i, [128, 69, 128], BF16) for i in range(2)]
                r_Vd = [fw.res() for _ in range(2)]
                NP = 6
                PT = [SB(s2, "PT%d" % i, [128, 512], BF16) for i in range(NP)]
                r_PT = [fw.res() for _ in range(NP)]
                accN_ = [SB(s2, "accN0", [128, NT], F32), accN7]
                accD_ = [SB(s2, "accD0", [128, NT], F32), accD7]
                r_accN_ = [fw.res() for _ in range(2)]
                r_accD_ = [fw.res() for _ in range(2)]
                mk = SB(s2, "mk", [128, 2, 256], BF16)
                r_mk = fw.res()
                ld(mk[:], masks_d, [r_mk], key="mk")
                psc = [PS(s2, "psc%d" % i, [128, 512], F32) for i in range(2)]
                r_psc = [fw.res() for _ in range(2)]
                pnum = [PS(s2, "pnum%d" % i, [128, 512], F32) for i in range(2)]
                r_pnum = [fw.res() for _ in range(2)]
                pden = [PS(s2, "pden%d" % i, [128, 512], F32) for i in range(2)]
                r_pden = [fw.res() for _ in range(2)]
                pvt = [PS(s2, "pvt%d" % i, [128, 8, 128], BF16) for i in range(2)]
                r_pvt = [fw.res() for _ in range(2)]
                cnt = {"sc": 0, "pt": 0, "vt": 0, "grp": 0}

                def load_head(h):
                    sl = h % 2
                    ld(vT[sl][:], VS[h], [r_v[sl]], key="hv%d" % sl)
                    ld(kT[sl][:], KS[h], [r_k[sl]], key="hk%d" % sl)
                    ld(qT[sl][:], QS[h], [r_q[sl]], key="hq%d" % sl)

                def load_ga(h):
                    ld(gaT[0][:], GAS[h], [r_ga[0]], key="hg0")

                vbase = {1: 0, 4: 17, 16: 37}

                def vidx(d, r, i):
                    nq = 16 // d
                    return vbase[d] + r * (nq + 1) + i

                def emit_vtrans(h):
                    sl = h % 2
                    tiles = []
                    for d in (1, 4, 16):
                        nq = 16 // d
                        for r in range(d):
                            for i in range(nq + 1):
                                t0 = 1024 // d - 64 + 128 * i
                                tiles.append((vidx(d, r, i), r + d * t0, d))
                    for g0 in range(0, len(tiles), 8):
                        grp = tiles[g0:g0 + 8]
                        b = cnt["vt"] % 2
                        cnt["vt"] += 1
                        for j, (vi, s0, d) in enumerate(grp):
                            tr(pvt[b][:, j, :], vT[sl][:, s0:s0 + 127 * d + 1:d], ident[:],
                               [r_v[sl], r_const], [r_pvt[b]], inc=(j == len(grp) - 1))
                        v0 = grp[0][0]
                        cp(act if (g0 // 8) % 2 == 0 else dve, Vd[sl][:, v0:v0 + len(grp), :],
                           pvt[b][:, 0:len(grp), :], [r_pvt[b]], [r_Vd[sl]])

                items = []
                load_head(0)
                load_ga(0)
                for h in range(8):
                    sl = h % 2
                    gbank = {}
                    tiles = []
                    for d in (1, 4, 16):
                        nq = 16 // d
                        for r in range(d):
                            for i in range(nq + 1):
                                t0 = 1024 // d - 64 + 128 * i
                                jlo, jhi = max(i - 1, 0), min(i, nq - 1)
                                if i == 0:
                                    msk = mk[:, 1, 128:256]
                                elif i == nq:
                                    msk = mk[:, 1, 0:128]
                                else:
                                    msk = mk[:, 0, :]
                                pv = []
                                for j in range(jlo, jhi + 1):
                                    if d == 16:
                                        gkey, col = (d, r // 4, 0), 128 * (r % 4)
                                    else:
                                        gkey, col = (d, r, j // 4), 128 * (j % 4)
                                    if gkey not in gbank:
                                        gbank[gkey] = cnt["grp"] % 2
                                        cnt["grp"] += 1
                                    pv.append((j, gbank[gkey], col))
                                tiles.append(dict(d=d, r=r, i=i, nq=nq, ks0=r + d * t0, jlo=jlo,
                                                  ncol=128 * (jhi - jlo + 1), q0=r + d * 128 * jlo, msk=msk, pv=pv))
                    npairs = (len(tiles) + 1) // 2
                    for pi in range(npairs):
                        pair = tiles[2 * pi:2 * pi + 2]
                        sc = cnt["sc"] % 2
                        cnt["sc"] += 1
                        p = cnt["pt"] % NP
                        cnt["pt"] += 1
                        c0 = 0
                        for tl in pair:
                            tl["c0"] = c0
                            c0 += tl["ncol"]
                        wtot = c0
                        last_pair = (pi == npairs - 1)

                        def s0(h=h, sl=sl, pi=pi, sc=sc, pair=pair):
                            if pi == 0 and h == 0:
                                emit_vtrans(0)
                            if pi == 3 and h + 1 < 8:
                                load_head(h + 1)
                            if pi == 15 and h + 1 < 8:
                                emit_vtrans(h + 1)
                            if pi == 20 and h >= 1:
                                load_ga(h)
                            if pi == 0:
                                cp(pool, q4[:].rearrange("p (r m) -> p r m", r=4),
                                   qT[sl][:].rearrange("p (m r) -> p r m", r=4), [r_q[sl]], [r_q4])
                                cp(pool, q16[:].rearrange("p (r m) -> p r m", r=16),
                                   qT[sl][:].rearrange("p (m r) -> p r m", r=16), [r_q[sl]], [r_q16])
                            if pi == 22 and h == 7:
                                prefetch_TB0()
                            for tl in pair:
                                d, ks0, q0, ncol, c0 = tl["d"], tl["ks0"], tl["q0"], tl["ncol"], tl["c0"]
                                if d == 1:
                                    qmov, r_qm = qT[sl][:, q0:q0 + ncol], r_q[sl]
                                elif d == 4:
                                    qb = 512 * tl["r"] + 128 * tl["jlo"]
                                    qmov, r_qm = q4[:, qb:qb + ncol], r_q4
                                else:
                                    qb = 128 * tl["r"]
                                    qmov, r_qm = q16[:, qb:qb + ncol], r_q16
                                mm(psc[sc][:, c0:c0 + ncol], kT[sl][:, ks0:ks0 + 127 * d + 1:d],
                                   qmov, True, False, [r_qm, r_k[sl]], [r_psc[sc]], inc=False)
                                mm(psc[sc][:, c0:c0 + ncol], ident[:], tl["msk"], False, True, [r_mk, r_const], [r_psc[sc]])

                        def s1(sc=sc, p=p, wtot=wtot):
                            actv(PT[p][:, 0:wtot], psc[sc][:, 0:wtot], AF.Exp, [r_psc[sc]], [r_PT[p]], scale=ATT_SCALE)

                        def s2_(h=h, sl=sl, p=p, pair=pair):
                            accN, accD, r_accN, r_accD = accN_[sl], accD_[sl], r_accN_[sl], r_accD_[sl]
                            for tl in pair:
                                d, r, i, jlo, c0 = tl["d"], tl["r"], tl["i"], tl["jlo"], tl["c0"]
                                for (j, bank, col) in tl["pv"]:
                                    pc = c0 + 128 * (j - jlo)
                                    first = (i == j)
                                    last = (i == j + 1)
                                    mm(pnum[bank][:, col:col + 128], Vd[sl][:, vidx(d, r, i), :], PT[p][:, pc:pc + 128],
                                       first, last, [r_Vd[sl], r_PT[p]], [r_pnum[bank]], inc=last)
                                    mm(pden[bank][:, col:col + 128], ones[:], PT[p][:, pc:pc + 128],
                                       first, last, [r_const, r_PT[p]], [r_pden[bank]], inc=last)
                                    if last:
                                        if d == 1 and j % 4 == 3:
                                            g = j // 4
                                            cp(dve, accN[:, 512 * g:512 * g + 512], pnum[bank][:], [r_pnum[bank]], [r_accN])
                                            cp(act, accD[:, 512 * g:512 * g + 512], pden[bank][:], [r_pden[bank]], [r_accD])
                                        elif d == 4 and j == 3:
                                            vN = accN[:].rearrange("p (m r) -> p r m", r=4)[:, r, :]
                                            vD = accD[:].rearrange("p (m r) -> p r m", r=4)[:, r, :]
                                            tt(dve, vN, pnum[bank][:], vN, ALU.add, [r_pnum[bank], r_accN], [r_accN])
                                            tt(dve, vD, pden[bank][:], vD, ALU.add, [r_pden[bank], r_accD], [r_accD])
                                        elif d == 16 and r % 4 == 3:
                                            r0 = r - 3
                                            vN = accN[:].rearrange("p (m r) -> p r m", r=16)[:, r0:r0 + 4, :]
                                            vD = accD[:].rearrange("p (m r) -> p r m", r=16)[:, r0:r0 + 4, :]
                                            pn = pnum[bank][:].rearrange("p (r m) -> p r m", r=4)
                                            pd = pden[bank][:].rearrange("p (r m) -> p r m", r=4)
                                            tt(dve, vN, pn, vN, ALU.add, [r_pnum[bank], r_accN], [r_accN])
                                            tt(dve, vD, pd, vD, ALU.add, [r_pden[bank], r_accD], [r_accD])

                        def f1(h=h, sl=sl):
                            accD, r_accD = accD_[sl], r_accD_[sl]
                            r_d1 = fw.res()
                            fw.dma(sp, DEN1[h:h + 1, :], accD[0:1, :], [r_accD], [r_d1], key="den1")
                            fw.dma(sp, rsm[sl][:], DEN1[h].rearrange("(p j) -> p j", j=16), [r_d1], [r_rsm[sl]], key="den2")

                        def f2(h=h, sl=sl):
                            accD, r_accD = accD_[sl], r_accD_[sl]
                            r_d2 = fw.res()
                            recip(rsm[sl][:], rsm[sl][:], [r_rsm[sl]], [r_rsm[sl]])
                            fw.dma(sp, DEN2[h].rearrange("(p j) -> p j", j=16), rsm[sl][:], [r_rsm[sl]], [r_d2], key="den3")
                            fw.dma(sp, accD[:], DEN2[h:h + 1, :].broadcast_to([128, NT]), [r_d2], [r_accD], key="den4")

                        def f3(h=h, sl=sl):
                            accN, accD, r_accN, r_accD = accN_[sl], accD_[sl], r_accN_[sl], r_accD_[sl]
                            tt(dve, accN[:], accN[:], accD[:], ALU.mult, [r_accN, r_accD], [r_accN])
                            tt(pool, mixT[:, h, :], accN[:], gaT[0][:], ALU.mult, [r_accN, r_ga[0]], [r_mix])

                        if last_pair and h == 7:
                            late_fin.extend([f1, f2, f3])
                            items.append([s0, s1, s2_])
                        else:
                            items.append([s0, s1, s2_] + ([f1, f2, f3] if last_pair else []))
                run_pipeline(items, [0, 0, 2, 4, 10, 16])
                fw.barrier()

            with ExitStack() as sCD:
                wo01 = SB(sCD, "wo01", [128, 16, 512], BF16)
                r_wo = [fw.res() for _ in range(4)]

                def load_wo(g, wt, lbase):
                    for hh in range(2):
                        c0 = 512 * g + 256 * hh
                        l0 = lbase + 256 * hh
                        ldw(wt[:, :, l0:l0 + 256], w_out[:, c0:c0 + 256].rearrange("(c p) n -> p c n", p=128),
                            [r_wo[g]], key="wo%d" % g)

                with ExitStack() as s2:
                    sdb = SB(s2, "sdb", [128, 16, 1024], BF16)
                    r_sdb = [fw.res() for _ in range(8)]
                    TB = [TB0, SB(s2, "TB1", [128, 16, 2, 256], BF16)]
                    r_TB = [r_TB0, [fw.res() for _ in range(2)]]
                    GF = [SB(s2, "GF%d" % i, [128, 8, 512], BF16) for i in range(2)]
                    r_GF = [fw.res() for _ in range(2)]
                    XS = SB(s2, "XS", [128, 8, 2, 256], BF16)
                    r_XS = [fw.res() for _ in range(8)]
                    csm = SB(s2, "csm", [128, 2, 2, 256], BF16)
                    wfb = SB(s2, "wfb", [128, 4, 2, 256], BF16)
                    r_cw = fw.res()
                    px = [PS(s2, "px%d" % i, [128, 2, 256], F32) for i in range(4)]
                    r_px = [fw.res() for _ in range(4)]
                    pf = [PS(s2, "pf%d" % i, [128, 512], F32) for i in range(2)]
                    r_pf = [fw.res() for _ in range(2)]
                    po = [PS(s2, "po%d" % i, [128, 512], F32) for i in range(2)]
                    r_po = [fw.res() for _ in range(2)]
                    cnt = {"x": 0, "f": 0, "o": 0}
                    slices = [(par, kb) for par in range(2) for kb in range(4)]

                    def load_TB(i):
                        par, kb = slices[i]
                        sl = i % 2
                        for cs in range(2):
                            ld(TB[sl][:, :, cs, :], dft_d[par, kb][:, :, cs, :], [r_TB[sl][cs]], key="TB%d%d" % (sl, cs))

                    def load_GF(i):
                        par, kb = slices[i]
                        sl = i % 2
                        ld(GF[sl][:], GFS[:, :, 512 * kb:512 * kb + 512].rearrange("c p k -> p c k"),
                           [r_GF[sl]], key="GF%d" % sl)

                    def load_sdb(par, cc):
                        ld(sdb[:, :, 128 * cc:128 * cc + 128],
                           SDS[par, :, :, 128 * cc:128 * cc + 128].rearrange("t p c -> p t c"),
                           [r_sdb[cc]], key="sdb%d" % cc)

                    load_sdb(0, 0)
                    ld(csm[:], csm_d, [r_cw], key="cw")
                    ldw(wfb[:], w_f.rearrange("g (c p) e -> p g c e", p=128), [r_cw], key="cw2")
                    for cc in range(1, 8):
                        load_sdb(0, cc)
                    load_GF(0)
                    load_wo(0, wo01, 0)
                    Gm = SB(s2, "Gm", [128, 4, 2, 2, 256], BF16)
                    r_Gm = fw.res()

                    gm_pieces = [(g, cs, ck) for g in range(4) for cs in range(2) for ck in range(2)]

                    def build_Gm(lo, hi):
                        for (g, cs, ck) in gm_pieces[lo:hi]:
                            b = cnt["f"] % 2
                            cnt["f"] += 1
                            for ek in range(2):
                                mm(pf[b][:, 0:256], csm[:, cs, ek, 128 * ck:128 * ck + 128], wfb[:, g, ek, :],
                                   ek == 0, ek == 1, [r_cw], [r_pf[b]], inc=(ek == 1))
                            cp(act, Gm[:, g, cs, ck, :], pf[b][:, 0:256], [r_pf[b]], [r_Gm])

                    for i, (par, kb) in enumerate(slices):
                        sl = i % 2
                        if i < 3:
                            late_fin[i]()
                        if i + 1 < len(slices):
                            load_TB(i + 1)
                            load_GF(i + 1)
                        for cc in range(8):
                            b = cnt["x"] % 4
                            cnt["x"] += 1
                            for cs in range(2):
                                for t in range(16):
                                    mm(px[b][:, cs, :], sdb[:, t, 128 * cc:128 * cc + 128], TB[sl][:, t, cs, :],
                                       t == 0, t == 15, [r_sdb[cc], r_TB[sl][cs]], [r_px[b]], inc=(t == 15 and cs == 1))
                            if par == 0 and kb == 3:
                                load_sdb(1, cc)
                            if i == 0 and cc >= 2:
                                build_Gm(3 * (cc - 2), min(16, 3 * (cc - 2) + 3))
                            cp(act if cc % 2 == 0 else dve, XS[:, cc, :, :], px[b][:], [r_px[b]], [r_XS[cc]])
                        for g in range(4):
                            for oc in range(2):
                                b = cnt["o"] % 2
                                cnt["o"] += 1
                                n = 0
                                for ck in range(2):
                                    for cs in range(2):
                                        mm(po[b][:, 0:256], Gm[:, g, cs, ck, 128 * oc:128 * oc + 128], XS[:, 2 * g + ck, cs, :],
                                           n == 0, n == 3, [r_Gm, r_XS[2 * g + ck]], [r_po[b]], inc=(n == 3))
                                        n += 1
                                ch = 2 * g + oc
                                k0 = 512 * kb + par
                                tt(dve, mixT[:, 8 + ch, k0:512 * kb + 512:2], po[b][:, 0:256], GF[sl][:, ch, par:512:2],
                                   ALU.mult, [r_po[b], r_GF[sl]], [r_mix])
                    fw.barrier()

                if debug:
                    for c in range(16):
                        stq(MIXS[c], mixT[:, c, :], [r_mix], key="dbg")
                    fw.barrier()

                with ExitStack() as s2:
                    wo123 = SB(s2, "wo123", [128, 16, 1536], BF16)
                    for g in range(1, 4):
                        load_wo(g, wo123, 512 * (g - 1))
                    NX = 4
                    xr = [SB(s2, "xr%d" % i, [128, 512], F32) for i in range(NX)]
                    r_xr = [fw.res() for _ in range(NX)]
                    yo = [SB(s2, "yo%d" % i, [128, 512], F32) for i in range(NX)]
                    r_yo = [fw.res() for _ in range(NX)]
                    py = [PS(s2, "py%d" % i, [128, 512], F32) for i in range(4)]
                    r_py = [fw.res() for _ in range(4)]
                    items = []
                    n = 0
                    for g in range(4):
                        wt = wo01 if g == 0 else wo123
                        l0 = 0 if g == 0 else 512 * (g - 1)
                        for t in range(16):
                            xi = n % NX
                            b = n % 4
                            n += 1
                            rows = slice(128 * t, 128 * t + 128)
                            cols = slice(512 * g, 512 * g + 512)

                            def s0(t=t, g=g, xi=xi, b=b, wt=wt, l0=l0, rows=rows, cols=cols):
                                ld(xr[xi][:], x_own[rows, cols], [r_xr[xi]], key="xr%d" % xi)
                                for c in range(16):
                                    mm(py[b][:], mixT[:, c, 128 * t:128 * t + 128], wt[:, c, l0:l0 + 512],
                                       c == 0, c == 15, [r_wo[g], r_mix], [r_py[b]], inc=(c == 15))

                            def s1(xi=xi, b=b, rows=rows, cols=cols):
                                tt(dve, yo[xi][:], py[b][:], xr[xi][:], ALU.add, [r_py[b], r_xr[xi]], [r_yo[xi]])
                                stq(y[rows, cols], yo[xi][:], [r_yo[xi]], key="yo%d" % xi)

                            items.append([s0, s1])
                    run_pipeline(items, [0, 1])
                    fw.barrier()
        fw.flush()
    return nc


def _rope_tables(pos):
    half = 16
    inv_freq = 1.0 / (500000.0 ** (np.arange(half, dtype=np.float64) / half))
    ang = pos.astype(np.float64)[None, :] * inv_freq[:, None]
    cos = np.cos(ang)
    sin = np.sin(ang)
    cosT = np.concatenate([cos, cos], axis=0)
    sinT = np.concatenate([-sin, sin], axis=0)
    return np.stack([cosT, sinT]).astype(np.float32)


def _dft_tables(kind):
    n = np.arange(2048, dtype=np.int64)
    j = np.arange(1024, dtype=np.int64)
    out = np.empty((2, 2, 2048, 1024), dtype=np.float32)
    for par in range(2):
        if kind == "A":
            N, k, sgn = 4096, 2 * j + par, 1.0
        elif kind == "B":
            N, k, sgn = 4096, 2048 + 2 * j + par, (1.0 if par == 0 else -1.0)
        else:
            N, k, sgn = 2048, 2 * j + par, 1.0
        ph = (n[:, None] * k[None, :]) % N
        th = 2.0 * np.pi * ph.astype(np.float64) / N
        out[par, 0] = sgn * np.cos(th)
        out[par, 1] = -sgn * np.sin(th)
    o = out.reshape(2, 2, 16, 128, 4, 256).transpose(0, 4, 3, 2, 1, 5)
    return np.ascontiguousarray(o).astype(BF)


def _csm_table(S):
    scale = 1.0 / math.sqrt(S * 256.0)
    c = np.arange(256, dtype=np.int64)
    ph = (c[:, None] * c[None, :]) % 256
    th = 2.0 * np.pi * ph.astype(np.float64) / 256
    m = np.stack([np.cos(th), np.sin(th)]) * scale
    o = m.reshape(2, 2, 128, 256).transpose(2, 0, 1, 3)
    return np.ascontiguousarray(o).astype(BF)


def _masks(kind):
    a = np.arange(128)[:, None]
    b = np.arange(256)[None, :]
    band = ((b - a) >= 0) & ((b - a) <= 128)
    left_valid = kind == "B"
    right_valid = kind == "A"
    edge = band.copy()
    if not left_valid:
        edge[:64, 128:256] = False
    if not right_valid:
        edge[64:, 0:128] = False
    m = np.stack([band, edge], axis=1).astype(np.float32)
    return ((m - 1.0) * 30000.0).astype(BF)


_NC_CACHE = {}
_CONST_CACHE = {}


def _consts(kind):
    if kind not in _CONST_CACHE:
        S = 4096 if kind in ("A", "B") else 2048
        _CONST_CACHE[kind] = dict(dft=_dft_tables(kind), csm=_csm_table(S), masks=_masks(kind))
    return _CONST_CACHE[kind]


def make_in_maps(x_prompt, x_sample, rms_gain, w_in, q_norm_gain, k_norm_gain, w_fourier, w_out):
    ident = np.eye(128, dtype=np.float32).astype(BF)
    ones = np.ones((128, 128), dtype=np.float32).astype(BF)
    psw = np.zeros((128, 128), dtype=np.float32)
    for m in range(16):
        psw[m + 16, m] = 1.0
        psw[m, m + 16] = 1.0
    psw = psw.astype(BF)
    shared = dict(
        rms_g=np.ascontiguousarray(rms_gain[0].reshape(1, DM)),
        w_in=np.ascontiguousarray(w_in[0]),
        qg=np.ascontiguousarray(q_norm_gain[0].reshape(128, 1)),
        kg=np.ascontiguousarray(k_norm_gain[0].reshape(128, 1)),
        w_f=np.ascontiguousarray(w_fourier[0]),
        w_out=np.ascontiguousarray(w_out[0]),
        ident=ident, ones=ones, pswap=psw,
    )
    zeros = np.zeros((NT, DM), dtype=np.float32)
    in_maps = []
    for c in range(8):
        if c < 4:
            b, half = c // 2, c % 2
            kind = "A" if half == 0 else "B"
            xo = x_prompt[b, 2048 * half:2048 * half + 2048]
            xt = x_prompt[b, 2048 * (1 - half):2048 * (1 - half) + 2048]
            h0 = 2048 if half == 0 else 1024
            xe = xt
            hoff = 1024 + (h0 - 2048 * (1 - half))
            pos_own = 2048 * half + np.arange(NT)
            pos_oth = np.concatenate([h0 + np.arange(1024), np.zeros(1024, dtype=np.int64)])
        else:
            kind = "S"
            xo = x_sample[c - 4]
            xe = zeros
            hoff = 1024
            pos_own = np.arange(NT)
            pos_oth = np.zeros(NT, dtype=np.int64)
        cst = _consts(kind)
        m = dict(shared)
        m.update(
            x_own=np.ascontiguousarray(xo), x_ext=np.ascontiguousarray(xe),
            hoff=np.array([[hoff]], dtype=np.int32),
            rope_own=_rope_tables(pos_own), rope_oth=_rope_tables(pos_oth),
            masks=cst["masks"], dft=cst["dft"], csm=cst["csm"],
        )
        in_maps.append(m)
    return in_maps


def kernel(x_prompt, x_sample, rms_gain, w_in, q_norm_gain, k_norm_gain, w_fourier, w_out):
    args = [np.asarray(a) for a in (x_prompt, x_sample, rms_gain, w_in, q_norm_gain, k_norm_gain, w_fourier, w_out)]
    in_maps = make_in_maps(*args)
    if "nc" not in _NC_CACHE:
        _NC_CACHE["nc"] = build()
    res = run_bass_kernel_spmd(_NC_CACHE["nc"], in_maps, core_ids=list(range(8)))
    outs = [np.asarray(r["y"], dtype=np.float32) for r in res.results]
    y_prompt = np.stack([np.concatenate([outs[0], outs[1]], axis=0),
                         np.concatenate([outs[2], outs[3]], axis=0)], axis=0)
    y_sample = np.stack(outs[4:8], axis=0)
    return (y_prompt, y_sample)
```
